# Optimizing a Trainium2 kernel written in Bass

```python
import jax, jax.numpy as jnp
from jax import lax
import numpy as np

D_MODEL = 1024
BATCH = 32
SEQ = 2048
DEPTH = 4

CHUNK = 64
N_MIXERS = 2
N_SB = (DEPTH + 1) // 2
N_SGU = DEPTH // 2
SB_HEADS = 16
SB_HEAD_DIM = D_MODEL // SB_HEADS
Q_BLOCK = 128
SGU_CHUNK = 2 * CHUNK
SGU_FFN = 2 * D_MODEL
SGU_GROUPS = 8
SGU_GROUP_W = SGU_FFN // SGU_GROUPS
MLP_HIDDEN = 4 * D_MODEL
EPS = 1e-6

kernel_name = "hybrid_stickbreak_sgu_encoder"


def rmsnorm(x, gain):
    x32 = x.astype(jnp.float32)
    y = x32 * lax.rsqrt(jnp.mean(x32 * x32, axis=-1, keepdims=True) + EPS)
    return (y * gain.astype(jnp.float32)).astype(x.dtype)


def stick_breaking_attention(q, k, v):
    seq = q.shape[2]
    scale = SB_HEAD_DIM ** -0.5
    outs = []
    for blk in range(seq // Q_BLOCK):
        t0 = blk * Q_BLOCK
        t1 = t0 + Q_BLOCK
        qb = q[:, :, t0:t1].astype(jnp.float32)
        kb = k[:, :, :t1].astype(jnp.float32)
        vb = v[:, :, :t1].astype(jnp.float32)
        z = jnp.einsum('bhtd,bhsd->bhts', qb, kb) * scale
        t_idx = t0 + jnp.arange(Q_BLOCK)[:, None]
        s_idx = jnp.arange(t1)[None, :]
        past = s_idx < t_idx
        log_beta = jax.nn.log_sigmoid(z)
        log_one_minus = jnp.where(past, log_beta - z, 0.0)
        suffix = lax.cumsum(log_one_minus, axis=3, reverse=True) - log_one_minus
        a = jnp.where(past, jnp.exp(log_beta + suffix), 0.0)
        outs.append(jnp.einsum('bhts,bhsd->bhtd', a, vb))
    return jnp.concatenate(outs, axis=2).astype(q.dtype)


def stick_breaking_mixer(h, w_qkv, w_o):
    b, s, _ = h.shape
    qkv = (h @ w_qkv).reshape(b, s, 3, SB_HEADS, SB_HEAD_DIM)
    q = jnp.transpose(qkv[:, :, 0], (0, 2, 1, 3))
    k = jnp.transpose(qkv[:, :, 1], (0, 2, 1, 3))
    v = jnp.transpose(qkv[:, :, 2], (0, 2, 1, 3))
    o = stick_breaking_attention(q, k, v)
    o = jnp.transpose(o, (0, 2, 1, 3)).reshape(b, s, D_MODEL)
    return o @ w_o


def spatial_gating_mixer(h, w_in, gain, w_s, b_s, w_out):
    b, s, _ = h.shape
    uv = jax.nn.gelu(h @ w_in)
    u, v = uv[..., :SGU_FFN], uv[..., SGU_FFN:]
    v = rmsnorm(v, gain)
    vc = v.reshape(b, s // SGU_CHUNK, SGU_CHUNK, SGU_GROUPS, SGU_GROUP_W)
    causal = jnp.tril(jnp.ones((SGU_CHUNK, SGU_CHUNK), dtype=bool))
    ws = jnp.where(causal[None], w_s, 0.0).astype(v.dtype)
    mixed = jnp.einsum('gts,bnsgc->bntgc', ws, vc) + jnp.transpose(b_s)[None, None, :, :, None]
    y = u * mixed.reshape(b, s, SGU_FFN)
    return y @ w_out


def squared_relu_mlp(h, w1, w2):
    return jnp.square(jax.nn.relu(h @ w1)) @ w2


def setup_inputs(seed: int = 0) -> dict:
    key = jax.random.key(seed)
    ks = jax.random.split(key, 14)
    f32 = jnp.float32
    nrm = lambda k, shape, scale: jax.random.normal(k, shape, f32) * scale
    return {
        "x": jax.random.normal(ks[0], (BATCH, SEQ, D_MODEL), f32),
        "norm_mix": 1.0 + nrm(ks[1], (DEPTH, D_MODEL), 0.02),
        "norm_mlp": 1.0 + nrm(ks[2], (DEPTH, D_MODEL), 0.02),
        "sb_wqkv": nrm(ks[3], (N_SB, D_MODEL, 3 * D_MODEL), D_MODEL ** -0.5),
        "sb_wo": nrm(ks[4], (N_SB, D_MODEL, D_MODEL), D_MODEL ** -0.5),
        "sgu_win": nrm(ks[5], (N_SGU, D_MODEL, 2 * SGU_FFN), D_MODEL ** -0.5),
        "sgu_gain": 1.0 + nrm(ks[6], (N_SGU, SGU_FFN), 0.02),
        "sgu_ws": nrm(ks[7], (N_SGU, SGU_GROUPS, SGU_CHUNK, SGU_CHUNK), SGU_CHUNK ** -0.5),
        "sgu_bs": 1.0 + nrm(ks[8], (N_SGU, SGU_GROUPS, SGU_CHUNK), 0.02),
        "sgu_wout": nrm(ks[9], (N_SGU, SGU_FFN, D_MODEL), SGU_FFN ** -0.5),
        "mlp_w1": nrm(ks[10], (DEPTH, D_MODEL, MLP_HIDDEN), D_MODEL ** -0.5),
        "mlp_w2": nrm(ks[11], (DEPTH, MLP_HIDDEN, D_MODEL), 0.5 * MLP_HIDDEN ** -0.5),
        "final_norm": 1.0 + nrm(ks[12], (D_MODEL,), 0.02),
    }


def reference(x, norm_mix, norm_mlp, sb_wqkv, sb_wo, sgu_win, sgu_gain, sgu_ws,
              sgu_bs, sgu_wout, mlp_w1, mlp_w2, final_norm):
    for i in range(DEPTH):
        h = rmsnorm(x, norm_mix[i])
        j = i // N_MIXERS
        if i % N_MIXERS == 0:
            x = x + stick_breaking_mixer(h, sb_wqkv[j], sb_wo[j])
        else:
            x = x + spatial_gating_mixer(h, sgu_win[j], sgu_gain[j], sgu_ws[j],
                                         sgu_bs[j], sgu_wout[j])
        x = x + squared_relu_mlp(rmsnorm(x, norm_mlp[i]), mlp_w1[i], mlp_w2[i])
    return rmsnorm(x, final_norm)
```

```python
import numpy as np
import ml_dtypes
from contextlib import ExitStack
import concourse.bass as bass
import concourse.mybir as mybir
from concourse.bass_utils import run_bass_kernel_spmd

F32 = mybir.dt.float32
BF16 = mybir.dt.bfloat16
AF = mybir.ActivationFunctionType
ALU = mybir.AluOpType

NCORES = 8
D = 1024
SEQ = 2048
DEPTH = 4
FF = 4096
SGF = 2048
EPS = 1e-6
ENGS = ("sp", "act", "dve", "pool", "pe")


class Sem:
    def __init__(self, h):
        self.h = h
        self.n = 0


class Buf:
    __slots__ = ("w", "r")

    def __init__(self):
        self.w = None
        self.r = []


class Ctx:
    pass


class Prog:
    def __init__(self, ctx):
        self.ctx = ctx
        self.q = {e: [] for e in ENGS}
        self.waited = ctx.waited
        self.pend = {e: ([], []) for e in ENGS}
        self.dma_rr = {e: 0 for e in ENGS}

    def _waits(self, eng, toks):
        best = {}
        w = self.waited[eng]
        pe_self = self.ctx.prog_sem["pe"] if eng == "pe" else None
        for t in toks:
            if t is None:
                continue
            s, v = t
            if s is pe_self:
                continue
            if v <= w.get(s, 0):
                continue
            if v > best.get(s, 0):
                best[s] = v
        out = []
        for s, v in best.items():
            w[s] = v
            out.append((s, v))
        return out

    def op(self, eng, fn, reads=(), writes=(), sig=True):
        toks = []
        for b in reads:
            toks.append(b.w)
        for b in writes:
            toks.append(b.w)
            toks.extend(b.r)
        waits = self._waits(eng, toks)
        pr, pw = self.pend[eng]
        pr.extend(reads)
        pw.extend(writes)
        inc = None
        tok = None
        if sig:
            s = self.ctx.prog_sem[eng]
            s.n += 1
            tok = (s, s.n)
            inc = (s.h, 1)
            for b in pr:
                b.r.append(tok)
            for b in pw:
                b.w = tok
                b.r = []
            pr.clear()
            pw.clear()
        self.q[eng].append((waits, fn, inc))
        return tok

    def dma(self, eng, out, in_, reads=(), writes=()):
        sems = self.ctx.dma_sems[eng]
        k = self.dma_rr[eng]
        self.dma_rr[eng] = (k + 1) % len(sems)
        s = sems[k]
        toks = [(s, s.n)] if s.n > 0 else []
        for b in reads:
            toks.append(b.w)
        for b in writes:
            toks.append(b.w)
            toks.extend(b.r)
        waits = self._waits(eng, toks)
        s.n += 16
        tok = (s, s.n)
        for b in reads:
            b.r.append(tok)
        for b in writes:
            b.w = tok
            b.r = []
        self.q[eng].append((waits, (lambda e, o=out, i=in_: e.dma_start(out=o, in_=i)), (s.h, 16)))
        return tok

    def fence(self, buf):
        toks = [buf.w] + list(buf.r)
        for e in ENGS:
            self.q[e].append((self._waits(e, toks), None, None))

    def mm(self, out, lhsT, rhs, start, stop, reads=(), writes=(), sig=False):
        return self.op("pe", lambda e: e.matmul(out, lhsT=lhsT, rhs=rhs, start=start, stop=stop),
                       reads, writes, sig)

    def tr(self, out, in_, ident, reads=(), writes=(), sig=False):
        return self.op("pe", lambda e: e.transpose(out=out, in_=in_, identity=ident), reads, writes, sig)

    def act(self, out, in_, func, reads=(), writes=(), **kw):
        return self.op("act", lambda e: e.activation(out=out, in_=in_, func=func, **kw), reads, writes)

    def tt(self, eng, out, in0, in1, op, reads=(), writes=()):
        return self.op(eng, lambda e: e.tensor_tensor(out=out, in0=in0, in1=in1, op=op), reads, writes)

    def ts(self, eng, out, in0, s1, s2, op0, op1=None, reads=(), writes=()):
        if op1 is None:
            return self.op(eng, lambda e: e.tensor_scalar(out=out, in0=in0, scalar1=s1, scalar2=None, op0=op0),
                           reads, writes)
        return self.op(eng, lambda e: e.tensor_scalar(out=out, in0=in0, scalar1=s1, scalar2=s2, op0=op0, op1=op1),
                       reads, writes)

    def stt(self, eng, out, in0, scalar, in1, op0, op1, reads=(), writes=()):
        return self.op(eng, lambda e: e.scalar_tensor_tensor(out=out, in0=in0, scalar=scalar, in1=in1,
                                                              op0=op0, op1=op1), reads, writes)

    def copy(self, eng, out, in_, reads=(), writes=()):
        if eng == "act":
            return self.op("act", lambda e: e.activation(out=out, in_=in_, func=AF.Copy), reads, writes)
        return self.op(eng, lambda e: e.tensor_copy(out=out, in_=in_), reads, writes)

    def run(self):
        ctx = self.ctx
        nc = ctx.nc
        finals = []
        for e in ENGS:
            s = ctx.prog_sem[e]
            if s.n > 0:
                finals.append((s, s.n))
            for d in ctx.dma_sems.get(e, []):
                if d.n > 0:
                    finals.append((d, d.n))
        for e in ENGS:
            self.q[e].append((self._waits(e, finals), None, None))
        q = self.q

        def mk(eng):
            def body(e):
                for waits, fn, inc in q[eng]:
                    for s, v in waits:
                        e.wait_ge(s.h, v)
                    if fn is not None:
                        ins = fn(e)
                        if inc is not None:
                            ins.then_inc(inc[0], inc[1])
            return body

        with nc.Block() as block:
            block.sync(mk("sp"))
            block.scalar(mk("act"))
            block.vector(mk("dve"))
            block.gpsimd(mk("pool"))
            block.tensor(mk("pe"))


def _uid(ctx, n):
    ctx.uid = getattr(ctx, "uid", 0) + 1
    return f"t{ctx.uid}_{n}"


def _rs(start, n):
    stop = start - n
    return slice(start, stop if stop >= 0 else None, -1)


class Converter:
    def __init__(self, P, stg32, stg16=None):
        self.P = P
        self.stg32 = stg32
        self.b32 = [Buf() for _ in stg32]
        self.stg16 = stg16
        self.b16 = [Buf() for _ in stg16] if stg16 else None
        self.i = 0
        self.engs = ("act", "dve")

    def go(self, src, shape, gain=None, dst_sb=None, dst_buf=None, dst_dram=None):
        P = self.P
        k = self.i
        self.i += 1
        s = k % len(self.stg32)
        n = int(np.prod(shape[1:]))
        s32 = self.stg32[s][:, 0:n]
        if len(shape) == 3:
            s32v = s32.rearrange("p (a b) -> p a b", b=shape[2])
        else:
            s32v = s32
        P.dma("sp", s32v, src, writes=[self.b32[s]])
        eng = self.engs[k % len(self.engs)]
        if dst_sb is not None:
            out = dst_sb
            wr = [dst_buf]
        else:
            s2 = k % len(self.stg16)
            o16 = self.stg16[s2][:, 0:n]
            out = o16.rearrange("p (a b) -> p a b", b=shape[2]) if len(shape) == 3 else o16
            wr = [self.b16[s2]]
        if gain is not None:
            if eng == "act":
                P.act(out, s32v, AF.Copy, reads=[self.b32[s]], writes=wr, scale=gain)
            else:
                P.ts(eng, out, s32v, gain, None, ALU.mult, reads=[self.b32[s]], writes=wr)
        else:
            P.copy(eng, out, s32v, reads=[self.b32[s]], writes=wr)
        if dst_sb is None:
            for dd, vv in dst_dram:
                P.dma("sp", dd, vv(self.stg16[s2][:, 0:n]), reads=[self.b16[s2]])


def emit_norm_sq_piece(P, xn, xn_b, sq, sq_b, k):
    P.act(sq[:, k, :], xn[:, k, :], AF.Square, reads=[xn_b], writes=[sq_b])


def emit_norm_stats(P, ctx, xn, xn_b, sq, sq_b, pst, pst_b, rstd, rstd_b, do_square=True):
    if do_square:
        P.act(sq[:].rearrange("p a b -> p (a b)"), xn[:].rearrange("p a b -> p (a b)"), AF.Square,
              reads=[xn_b], writes=[sq_b])
    for k in range(8):
        P.mm(pst[:], ctx.onesb[:], sq[:, k, :], k == 0, k == 7, reads=[sq_b], writes=[pst_b], sig=(k == 7))
    P.act(rstd[:], pst[:], AF.Sqrt, reads=[pst_b], writes=[rstd_b], bias=ctx.epsc[:, 0:1], scale=1.0 / D)
    P.op("dve", lambda e: e.reciprocal(out=rstd[:], in_=rstd[:]), reads=[rstd_b], writes=[rstd_b])


def phase_prep(ctx):
    nc = ctx.nc
    with ExitStack() as st:
        sb = lambda n, s, d: st.enter_context(nc.sbuf_tensor(_uid(ctx, n), s, d))
        P = Prog(ctx)
        cb = Buf()
        P.dma("sp", ctx.identf[:], ctx.d_identf[:, :], writes=[cb])
        P.dma("sp", ctx.identb[:], ctx.d_identb[:, :], writes=[cb])
        P.dma("sp", ctx.onesb[:], ctx.d_onesb[:, :], writes=[cb])
        P.dma("sp", ctx.maskneg[:], ctx.d_maskneg[:, :], writes=[cb])
        P.dma("sp", ctx.tril[:], ctx.d_tril[:, :], writes=[cb])
        P.op("dve", lambda e: e.memset(ctx.zero1[:], 0.0), writes=[cb])
        P.op("dve", lambda e: e.memset(ctx.epsc[:], EPS), writes=[cb])
        P.op("dve", lambda e: e.memset(ctx.mhalf[:], -0.5), writes=[cb])
        g72 = sb("g72", [72, 128], F32)
        gb = Buf()
        P.dma("sp", g72[0:32, :], ctx.d_norm_mix.rearrange("l (k p) -> (l k) p", p=128), writes=[gb])
        P.dma("sp", g72[32:64, :], ctx.d_norm_mlp.rearrange("l (k p) -> (l k) p", p=128), writes=[gb])
        P.dma("sp", g72[64:72, :], ctx.d_final_norm.rearrange("(k p) -> k p", p=128), writes=[gb])
        pg = st.enter_context(nc.psum_tensor(_uid(ctx, "pg"), [128, 72], F32))
        pgb = Buf()
        P.tr(pg[:], g72[:], ctx.identf[0:72, 0:72], reads=[gb, cb], writes=[pgb], sig=True)
        P.copy("dve", ctx.gT[:], pg[:], reads=[pgb], writes=[cb])
        P.fence(cb)

        stg32 = [sb(f"pstg32_{k}", [128, 4096], F32) for k in range(3)]
        stg16 = [sb(f"pstg16_{k}", [128, 4096], BF16) for k in range(3)]
        cv = Converter(P, stg32, stg16)
        cv.b32 = [Buf() for _ in stg32]
        for it in ctx.prep_items:
            it(cv)

        xin = [sb(f"xin{k}", [128, 1024], F32) for k in range(6)]
        xin_b = [Buf() for _ in range(6)]
        xts = [sb(f"xts{k}", [128, 8, 512], F32) for k in range(2)]
        xts_b = [Buf() for _ in range(2)]
        ptx = [st.enter_context(nc.psum_tensor(_uid(ctx, f"ptx{k}"), [128, 8, 128], F32)) for k in range(2)]
        ptx_b = [Buf() for _ in range(2)]
        ntt = ctx.TOK // 128
        def xload(t):
            P.dma("sp", xin[t % 6][:], ctx.d_x[t * 128:(t + 1) * 128, :], writes=[xin_b[t % 6]])

        for t in range(min(5, ntt)):
            xload(t)
        for t in range(ntt):
            s = t % 6
            if t + 5 < ntt:
                xload(t + 5)
            pp = t % 2
            for k in range(8):
                P.tr(ptx[pp][:, k, :], xin[s][:, k * 128:(k + 1) * 128], ctx.identf[:],
                     reads=[xin_b[s], cb], writes=[ptx_b[pp]], sig=(k == 7))
            big = (t // 4) % 2
            P.copy("act" if t % 2 == 0 else "dve", xts[big][:, :, (t % 4) * 128:(t % 4 + 1) * 128], ptx[pp][:],
                   reads=[ptx_b[pp]], writes=[xts_b[big]])
            if t % 4 == 3:
                t0 = (t // 4) * 512
                P.dma("sp", ctx.xT[:, :, t0:t0 + 512], xts[big][:], reads=[xts_b[big]])
        P.run()


def phase_mlp(ctx, i):
    nc = ctx.nc
    with ExitStack() as st:
        sb = lambda n, s, d: st.enter_context(nc.sbuf_tensor(_uid(ctx, n), s, d))
        ps = lambda n, s, d: st.enter_context(nc.psum_tensor(_uid(ctx, n), s, d))
        P = Prog(ctx)
        w1 = sb("w1", [128, 8, 4096], BF16)
        w1_b = Buf()
        w2s = [sb(f"w2s{k}", [128, 32, 128], BF16) for k in range(3)]
        w2s_b = [Buf() for _ in range(3)]
        xn = sb("xn", [128, 8, 512], F32)
        xn_b = Buf()
        sq = sb("sq", [128, 8, 512], BF16)
        sq_b = Buf()
        rstd = sb("rstd", [128, 512], F32)
        rstd_b = Buf()
        hT = [sb(f"hT{k}", [128, 8, 512], BF16) for k in range(2)]
        hT_b = [Buf() for _ in range(2)]
        hid = sb("hid", [128, 32, 512], BF16)
        hid_b = Buf()
        rt = [sb(f"rt{k}", [128, 512], BF16) for k in range(2)]
        rt_b = [Buf() for _ in range(2)]
        xres = [sb(f"xres{k}", [128, 512], F32) for k in range(3)]
        xres_b = [Buf() for _ in range(3)]
        pst = ps("pst", [128, 512], F32)
        pst_b = Buf()
        ph = [ps(f"ph{k}", [128, 512], F32) for k in range(2)]
        ph_b = [Buf() for _ in range(2)]
        po = [ps(f"po{k}", [128, 512], F32) for k in range(2)]
        po_b = [Buf() for _ in range(2)]

        w1_bs = [Buf() for _ in range(4)]
        P.dma("sp", xn[:], ctx.xT[:, :, 0:512], writes=[xn_b])
        for q4 in range(4):
            P.dma("sp", w1[:, 2 * q4:2 * q4 + 2, :], ctx.w1b[i][:, 2 * q4:2 * q4 + 2, :], writes=[w1_bs[q4]])

        NT = ctx.TOK // 512

        def norm_front(n):
            P.dma("sp", xn[:], ctx.xT[:, :, n * 512:(n + 1) * 512], writes=[xn_b])

        def norm_back(n, do_square=True):
            emit_norm_stats(P, ctx, xn, xn_b, sq, sq_b, pst, pst_b, rstd, rstd_b, do_square=do_square)
            for k in range(8):
                P.tt("dve", hT[n % 2][:, k, :], xn[:, k, :], rstd[:], ALU.mult,
                     reads=[xn_b, rstd_b], writes=[hT_b[n % 2]])

        norm_back(0)
        gidx = 0
        for n in range(NT):
            tok0 = n * 512
            if n + 1 < NT:
                norm_front(n + 1)
            h = hT[n % 2]
            for c in range(32):
                p = c % 2
                for k in range(8):
                    P.mm(ph[p][:], w1[:, k, c * 128:(c + 1) * 128], h[:, k, :], k == 0, k == 7,
                         reads=w1_bs + [hT_b[n % 2]], writes=[ph_b[p]], sig=(k == 7))
                P.act(rt[p][:], ph[p][:], AF.Relu, reads=[ph_b[p]], writes=[rt_b[p]])
                P.tt("dve", hid[:, c, :], rt[p][:], rt[p][:], ALU.mult, reads=[rt_b[p]], writes=[hid_b])
                if n + 1 < NT and 8 <= c < 24 and c % 2 == 0:
                    emit_norm_sq_piece(P, xn, xn_b, sq, sq_b, (c - 8) // 2)
            if n + 1 < NT:
                norm_back(n + 1, do_square=False)
            for f in range(8):
                s = gidx % 3
                gidx += 1
                P.dma("sp", w2s[s][:], ctx.w2f[i][:, f, :, :], writes=[w2s_b[s]])
                P.dma("sp", xres[s][:], ctx.xT[:, f, tok0:tok0 + 512], writes=[xres_b[s]])
                p = f % 2
                for c in range(32):
                    P.mm(po[p][:], w2s[s][:, c, :], hid[:, c, :], c == 0, c == 31,
                         reads=[w2s_b[s], hid_b], writes=[po_b[p]], sig=(c == 31))
                P.tt("dve", xres[s][:], xres[s][:], po[p][:], ALU.add, reads=[po_b[p], xres_b[s]],
                     writes=[xres_b[s]])
                P.dma("pool", ctx.xT[:, f, tok0:tok0 + 512], xres[s][:], reads=[xres_b[s]])
        P.run()


def phase_sgu(ctx, i, j):
    nc = ctx.nc
    with ExitStack() as st:
        sb = lambda n, s, d: st.enter_context(nc.sbuf_tensor(_uid(ctx, n), s, d))
        ps = lambda n, s, d: st.enter_context(nc.psum_tensor(_uid(ctx, n), s, d))
        P = Prog(ctx)
        win = sb("win", [128, 8, 4096], BF16)
        win_b = Buf()
        wouts = [sb(f"wouts{k}", [128, 16, 128], BF16) for k in range(3)]
        wouts_b = [Buf() for _ in range(3)]
        wsT = sb("wsT", [128, 8, 128], BF16)
        gain_bc = sb("gain_bc", [128, 2048], F32)
        b_bc = sb("b_bc", [128, 16, 128], F32)
        cst_b = Buf()
        xn = sb("xn", [128, 8, 512], F32)
        xn_b = Buf()
        sq = sb("sq", [128, 8, 512], BF16)
        sq_b = Buf()
        rstd = sb("rstd", [128, 512], F32)
        rstd_b = Buf()
        hT = [sb(f"hT{k}", [128, 8, 512], BF16) for k in range(2)]
        hT_b = [Buf() for _ in range(2)]
        uT = sb("uT", [128, 16, 512], BF16)
        uT_b = Buf()
        vgs = [sb(f"vg{k}", [128, 2048], F32) for k in range(2)]
        vgs_b = [Buf() for _ in range(2)]
        vn = [sb(f"vn{k}", [128, 2048], BF16) for k in range(2)]
        vn_b = [Buf() for _ in range(2)]
        ssq = [sb(f"ssq{k}", [128, 1], F32) for k in range(2)]
        ssq_b = [Buf() for _ in range(2)]
        yT = sb("yT", [128, 16, 512], BF16)
        yT_b = Buf()
        xres = [sb(f"xres{k}", [128, 512], F32) for k in range(3)]
        xres_b = [Buf() for _ in range(3)]
        stg32 = [sb(f"stg32_{k}", [128, 128], F32) for k in range(2)]
        stg_b = [Buf() for _ in range(2)]
        pst = ps("pst", [128, 512], F32)
        pst_b = Buf()
        pu = [ps(f"pu{k}", [128, 512], F32) for k in range(2)]
        pu_b = [Buf() for _ in range(2)]
        pm = [ps(f"pm{k}", [128, 4, 128], F32) for k in range(2)]
        pm_b = [Buf() for _ in range(2)]
        po = [ps(f"po{k}", [128, 512], F32) for k in range(2)]
        po_b = [Buf() for _ in range(2)]

        P.dma("sp", gain_bc[:], ctx.d_sgain[j, :].partition_broadcast(128), writes=[cst_b])
        bsrc = ctx.d_sbs[j, :, :].rearrange("g t -> (g t)").partition_broadcast(128).rearrange("p (g t) -> p g t", t=128)
        P.dma("sp", b_bc[:, 0:16:2, :], bsrc, writes=[cst_b])
        P.dma("sp", b_bc[:, 1:16:2, :], bsrc, writes=[cst_b])
        for g in range(8):
            s32 = stg32[g % 2]
            sbuf_b = stg_b[g % 2]
            P.dma("sp", s32[:, 0:128], ctx.d_sws[j, g, :, :], writes=[sbuf_b])
            P.tt("dve", s32[:, 0:128], s32[:, 0:128], ctx.tril[:], ALU.mult, reads=[sbuf_b], writes=[sbuf_b])
            P.tr(pu[g % 2][:, 0:128], s32[:, 0:128], ctx.identf[:], reads=[sbuf_b], writes=[pu_b[g % 2]], sig=True)
            P.copy("act", wsT[:, g, :], pu[g % 2][:, 0:128], reads=[pu_b[g % 2]], writes=[cst_b])
        win_bs = [Buf() for _ in range(4)]
        for q4 in range(4):
            P.dma("sp", win[:, 2 * q4:2 * q4 + 2, :], ctx.winb[j][:, 2 * q4:2 * q4 + 2, :], writes=[win_bs[q4]])

        NT = ctx.TOK // 512

        def norm_front(n):
            P.dma("sp", xn[:], ctx.xT[:, :, n * 512:(n + 1) * 512], writes=[xn_b])

        def norm_back(n, do_square=True):
            emit_norm_stats(P, ctx, xn, xn_b, sq, sq_b, pst, pst_b, rstd, rstd_b, do_square=do_square)
            for k in range(8):
                P.tt("dve", hT[n % 2][:, k, :], xn[:, k, :], rstd[:], ALU.mult,
                     reads=[xn_b, rstd_b], writes=[hT_b[n % 2]])

        norm_front(0)
        norm_back(0)
        gidx = 0
        pui = 0
        pmi = 0
        for n in range(NT):
            tok0 = n * 512
            if n + 1 < NT:
                norm_front(n + 1)
            h = hT[n % 2]
            hb = hT_b[n % 2]
            for c in range(16):
                p = pui % 2
                pui += 1
                for k in range(8):
                    P.mm(pu[p][:], win[:, k, c * 128:(c + 1) * 128], h[:, k, :], k == 0, k == 7,
                         reads=win_bs + [hb], writes=[pu_b[p]], sig=(k == 7))
                P.act(uT[:, c, :], pu[p][:], AF.Gelu_apprx_tanh, reads=[pu_b[p]], writes=[uT_b])
                if n + 1 < NT and 4 <= c < 12:
                    emit_norm_sq_piece(P, xn, xn_b, sq, sq_b, c - 4)

            def v_stage(jj):
                nonlocal pui
                vg, vg_b = vgs[jj % 2], vgs_b[jj % 2]
                for q4 in range(4):
                    p = pui % 2
                    pui += 1
                    for k in range(8):
                        P.mm(pu[p][:], h[:, k, jj * 128:(jj + 1) * 128],
                             win[:, k, 2048 + q4 * 512:2048 + (q4 + 1) * 512], k == 0, k == 7,
                             reads=win_bs + [hb], writes=[pu_b[p]], sig=(k == 7))
                    P.act(vg[:, q4 * 512:(q4 + 1) * 512], pu[p][:], AF.Gelu_apprx_tanh, reads=[pu_b[p]],
                          writes=[vg_b])

            def v_chain(jj):
                v2 = jj % 2
                vg, vg_b = vgs[jj % 2], vgs_b[jj % 2]
                P.op("dve", lambda e: e.memset(ssq[v2][:], 0.0), writes=[ssq_b[v2]])
                P.op("dve", lambda e: e.scalar_tensor_tensor(out=vn[v2][:], in0=vg[:], scalar=1.0, in1=vg[:],
                                                             op0=ALU.mult, op1=ALU.mult, accum_out=ssq[v2][:]),
                     reads=[vg_b, ssq_b[v2]], writes=[vn_b[v2], ssq_b[v2]])
                P.ts("dve", ssq[v2][:], ssq[v2][:], 1.0 / SGF, EPS, ALU.mult, ALU.add, reads=[ssq_b[v2]],
                     writes=[ssq_b[v2]])
                P.tt("pool", ssq[v2][:], ssq[v2][:], ctx.mhalf[:, 0:1], ALU.pow, reads=[ssq_b[v2]],
                     writes=[ssq_b[v2]])
                P.stt("dve", vn[v2][:], vg[:], ssq[v2][:, 0:1], gain_bc[:], ALU.mult, ALU.mult,
                      reads=[vg_b, ssq_b[v2], cst_b], writes=[vn_b[v2]])

            def mix_stage(jj):
                nonlocal pmi
                v2 = jj % 2
                for qq in range(4):
                    p = pmi % 2
                    pmi += 1
                    for cc in range(4):
                        c = qq * 4 + cc
                        g = c // 2
                        P.mm(pm[p][:, cc, :], vn[v2][:, c * 128:(c + 1) * 128], wsT[:, g, :], True, True,
                             reads=[vn_b[v2], cst_b], writes=[pm_b[p]], sig=(cc == 3))
                    yv = yT[:, qq * 4:(qq + 1) * 4, jj * 128:(jj + 1) * 128]
                    P.tt("dve", yv, pm[p][:], b_bc[:, qq * 4:(qq + 1) * 4, :], ALU.add,
                         reads=[pm_b[p], cst_b], writes=[yT_b])
                    P.tt("dve", yv, yv, uT[:, qq * 4:(qq + 1) * 4, jj * 128:(jj + 1) * 128], ALU.mult,
                         reads=[yT_b, uT_b], writes=[yT_b])

            v_stage(0)
            v_chain(0)
            for jj in range(1, 4):
                v_stage(jj)
                mix_stage(jj - 1)
                v_chain(jj)
            mix_stage(3)
            if n + 1 < NT:
                norm_back(n + 1, do_square=False)
            for f in range(8):
                s = gidx % 3
                gidx += 1
                P.dma("sp", wouts[s][:], ctx.woutf[j][:, f, :, :], writes=[wouts_b[s]])
                P.dma("sp", xres[s][:], ctx.xT[:, f, tok0:tok0 + 512], writes=[xres_b[s]])
                p = f % 2
                for c in range(16):
                    P.mm(po[p][:], wouts[s][:, c, :], yT[:, c, :], c == 0, c == 15,
                         reads=[wouts_b[s], yT_b], writes=[po_b[p]], sig=(c == 15))
                P.tt("dve", xres[s][:], xres[s][:], po[p][:], ALU.add, reads=[po_b[p], xres_b[s]],
                     writes=[xres_b[s]])
                P.dma("pool", ctx.xT[:, f, tok0:tok0 + 512], xres[s][:], reads=[xres_b[s]])
        P.run()


def phase_attn(ctx, i, j):
    nc = ctx.nc
    with ExitStack() as st:
        sb = lambda n, s, d: st.enter_context(nc.sbuf_tensor(_uid(ctx, n), s, d))
        ps = lambda n, s, d: st.enter_context(nc.psum_tensor(_uid(ctx, n), s, d))
        P = Prog(ctx)
        wo = sb("wo", [128, 8, 1024], BF16)
        wo_b = Buf()
        wp = [sb(f"wp{k}", [128, 3, 8, 128], BF16) for k in range(2)]
        wp_b = [Buf() for _ in range(2)]
        hrT = sb("hrT", [128, 8, 2048], BF16)
        hrT_b = Buf()
        qT = [sb(f"qT{k}", [128, 2048], BF16) for k in range(2)]
        kT = [sb(f"kT{k}", [128, 2048], BF16) for k in range(2)]
        vr = [sb(f"vr{k}", [128, 16, 128], BF16) for k in range(2)]
        q_b = [Buf() for _ in range(2)]
        k_b = [Buf() for _ in range(2)]
        v_b = [Buf() for _ in range(2)]
        oT = sb("oT", [128, 8, 2048], BF16)
        oT_b = Buf()
        G = [sb(f"G{k}", [128, 1024], F32) for k in range(3)]
        G_b = [Buf() for _ in range(3)]
        Pb = [sb(f"Pb{k}", [128, 2049], F32) for k in range(2)] + [sb("Pb2", [128, 1025], F32)]
        Pb_b = [Buf() for _ in range(3)]
        Pbh_b = [Buf() for _ in range(3)]
        A = [sb(f"A{k}", [128, 1024], BF16) for k in range(2)]
        A_b = [Buf() for _ in range(2)]
        AT = [sb(f"AT{k}", [128, 8, 128], BF16) for k in range(2)]
        AT_b = [Buf() for _ in range(2)]
        xc = [sb(f"xc{k}", [128, 8, 512], F32) for k in range(2)]
        xc_b = [Buf() for _ in range(2)]
        sq = sb("sq", [128, 8, 512], BF16)
        sq_b = Buf()
        rstd = sb("rstd", [128, 512], F32)
        rstd_b = Buf()
        xres = [sb(f"xres{k}", [128, 512], F32) for k in range(3)]
        xres_b = [Buf() for _ in range(3)]
        S = [ps(f"S{k}", [128, 1024], F32) for k in range(2)]
        S_b = [Buf() for _ in range(2)]
        ATp = ps("ATp", [128, 8, 128], BF16)
        ATp_b = Buf()
        oTp = ps("oTp", [128, 4, 128], F32)
        oTp_b = Buf()
        oTs_b = [Buf() for _ in range(4)]
        pp = [ps(f"pp{k}", [128, 512], F32) for k in range(2)]
        pp_b = [Buf() for _ in range(2)]
        ppi = 0

        stg = [xc[0][:].rearrange("p a b -> p (a b)"), xc[1][:].rearrange("p a b -> p (a b)")]

        class _S:
            def __init__(self, ap):
                self.ap = ap

            def __getitem__(self, idx):
                return self.ap[idx]

        cv = Converter(P, [_S(stg[0]), _S(stg[1])])
        cv.b32 = xc_b
        for kc2 in range(2):
            src = ctx.d_wo[j, kc2 * 512:(kc2 + 1) * 512, :].rearrange("(c p) n -> p c n", p=128)
            cv.go(src, [128, 4, 1024], dst_sb=wo[:, kc2 * 4:(kc2 + 1) * 4, :], dst_buf=wo_b)
        for k in range(3):
            P.op("dve", lambda e, k=k: e.memset(Pb[k][:, 0:1], 1.0), writes=[Pb_b[k], Pbh_b[k]])

        NSEQ = ctx.TOK // SEQ
        xi = 0

        def norm_items(s_):
            base = s_ * SEQ
            items = []

            def mk(tc):
                def f():
                    nonlocal xi
                    b = xi % 2
                    xi += 1
                    P.dma("sp", xc[b][:], ctx.xT[:, :, base + tc * 512:base + (tc + 1) * 512], writes=[xc_b[b]])
                    emit_norm_stats(P, ctx, xc[b], xc_b[b], sq, sq_b, pp[0], pp_b[0], rstd, rstd_b)
                    for k in range(8):
                        P.tt("dve", hrT[:, k, _rs(SEQ - 1 - tc * 512, 512)], xc[b][:, k, :], rstd[:], ALU.mult,
                             reads=[xc_b[b], rstd_b], writes=[hrT_b])
                return f
            for tc in range(4):
                items.append(mk(tc))
            return items

        def proj_items(pr, w):
            items = []

            def ld():
                P.dma("sp", wp[w][:], ctx.wqkvp[j][:, pr, :, :, :], writes=[wp_b[w]])

            def qk_item(which, tcq):
                def f():
                    nonlocal ppi
                    dstT, dst_b = (qT[w], q_b[w]) if which == 0 else (kT[w], k_b[w])
                    p = ppi % 2
                    ppi += 1
                    for k in range(8):
                        P.mm(pp[p][:], wp[w][:, which, k, :], hrT[:, k, tcq * 512:(tcq + 1) * 512],
                             k == 0, k == 7, reads=[wp_b[w], hrT_b], writes=[pp_b[p]], sig=(k == 7))
                    P.copy("act", dstT[:, tcq * 512:(tcq + 1) * 512], pp[p][:], reads=[pp_b[p]],
                           writes=[dst_b])
                return f

            def v_item(t4):
                def f():
                    nonlocal ppi
                    p = ppi % 2
                    ppi += 1
                    for tt_ in range(4):
                        tile = t4 * 4 + tt_
                        for k in range(8):
                            P.mm(pp[p][:, tt_ * 128:(tt_ + 1) * 128], hrT[:, k, tile * 128:(tile + 1) * 128],
                                 wp[w][:, 2, k, :], k == 0, k == 7, reads=[wp_b[w], hrT_b],
                                 writes=[pp_b[p]], sig=(k == 7 and tt_ == 3))
                    P.copy("act", vr[w][:, t4 * 4:(t4 + 1) * 4, :],
                           pp[p][:].rearrange("p (a b) -> p a b", b=128), reads=[pp_b[p]], writes=[v_b[w]])
                return f

            items.append(ld)
            for which in range(2):
                for tcq in range(4):
                    items.append(qk_item(which, tcq))
            for t4 in range(4):
                items.append(v_item(t4))
            return items

        gi = 0

        def wo_items(s_):
            base = s_ * SEQ
            items = []

            def mk(tcr, f):
                def g():
                    nonlocal ppi, gi
                    t_fwd = base + SEQ - 512 * (tcr + 1)
                    p = ppi % 2
                    ppi += 1
                    s3 = gi % 3
                    gi += 1
                    P.dma("sp", xres[s3][:], ctx.xT[:, f, t_fwd:t_fwd + 512], writes=[xres_b[s3]])
                    for pr in range(8):
                        P.mm(pp[p][:], wo[:, pr, f * 128:(f + 1) * 128], oT[:, pr, tcr * 512:(tcr + 1) * 512],
                             pr == 0, pr == 7, reads=[wo_b, oT_b], writes=[pp_b[p]], sig=(pr == 7))
                    P.tt("dve", xres[s3][:], xres[s3][:], pp[p][:, ::-1], ALU.add,
                         reads=[pp_b[p], xres_b[s3]], writes=[xres_b[s3]])
                    P.dma("pool", ctx.xT[:, f, t_fwd:t_fwd + 512], xres[s3][:], reads=[xres_b[s3]])
                return g
            for tcr in range(4):
                for f in range(8):
                    items.append(mk(tcr, f))
            return items

        chunks = []
        blk = 0
        bigc = 0
        for s_ in range(NSEQ):
            for pr in range(8):
                kp = 0
                for grp in range(1):
                    for hl in range(2):
                        for B in BLOCK_ORDER:
                            u0 = B * 128
                            nk = SEQ - u0
                            off = 0
                            while off < nk:
                                n_c = min(1024, nk - off)
                                chunks.append(dict(s=s_, pr=pr, w=pr % 2, grp=grp, hl=hl, B=B, u0=u0 + off,
                                                   off=off, n=n_c, first=(off == 0), last=(off + n_c == nk),
                                                   pb=(2 if nk <= 1024 else bigc % 2), slot=blk % 4, kp=kp))
                                kp += 1
                                off += n_c
                            blk += 1
                            if nk > 1024:
                                bigc += 1

        def stage1(n):
            ch = chunks[n]
            hl, B, u0, nn, w = ch["hl"], ch["B"], ch["u0"], ch["n"], ch["w"]
            s2 = n % 2
            hs = slice(64 * hl, 64 * hl + 64)
            qsl = qT[w][hs, B * 128:(B + 1) * 128]
            Sx = S[s2]
            mms = []
            if ch["first"]:
                mms.append((Sx[:, 0:128], ctx.identb[:], ctx.maskneg[:], True, False))
                mms.append((Sx[:, 0:128], qsl, kT[w][hs, u0:u0 + 128], False, True))
                pieces = [(128, min(nn, 512)), (512, nn)]
            else:
                pieces = [(0, min(nn, 512)), (512, nn)]
            for (a, b_) in pieces:
                if b_ > a:
                    mms.append((Sx[:, a:b_], qsl, kT[w][hs, u0 + a:u0 + b_], True, True))
            for d_ in range(N_DUMMY):
                P.mm(Sx[:, 0:512], ctx.identb[:], wo[:, d_ % 8, 0:512], True, True, reads=[wo_b], writes=[S_b[s2]])
            for idx, (o_, l_, r_, st_, sp_) in enumerate(mms):
                P.mm(o_, l_, r_, st_, sp_, reads=[q_b[w], k_b[w]], writes=[S_b[s2]],
                     sig=(idx == len(mms) - 1))

        def stage1s(n):
            ch = chunks[n]
            nn = ch["n"]
            s2 = n % 2
            g3 = n % 3
            Sx = S[s2]
            P.act(G[g3][:, 0:nn], Sx[:, 0:nn], AF.Sigmoid, reads=[S_b[s2]], writes=[G_b[g3]], scale=-0.125)
            pb = ch["pb"]
            off = ch["off"]
            init = 1.0 if ch["first"] else Pb[pb][:, off:off + 1]
            P.op("dve", lambda e: e.tensor_tensor_scan(
                out=Pb[pb][:, 1 + off:1 + off + nn], data0=G[g3][:, 0:nn],
                data1=ctx.zero1[:, 0:1].to_broadcast([128, nn]), initial=init,
                op0=ALU.mult, op1=ALU.add),
                reads=([G_b[g3]] if ch["first"] else [G_b[g3], Pb_b[pb]]),
                writes=([Pb_b[pb]] if ch["first"] else [Pbh_b[pb]]))

        def stage1b(n):
            ch = chunks[n]
            nn, pb, off = ch["n"], ch["pb"], ch["off"]
            s2 = n % 2
            P.tt("dve" if nn <= SUB_DVE_MAX else "pool", A[s2][:, 0:nn], Pb[pb][:, off:off + nn],
                 Pb[pb][:, off + 1:off + 1 + nn],
                 ALU.subtract, reads=([Pb_b[pb]] if ch["first"] else [Pb_b[pb], Pbh_b[pb]]),
                 writes=[A_b[s2]])

        def stage2(n):
            ch = chunks[n]
            nn = ch["n"]
            s2 = n % 2
            nt = nn // 128
            for t in range(nt):
                P.tr(ATp[:, t, :], A[s2][:, t * 128:(t + 1) * 128], ctx.identb[:], reads=[A_b[s2]],
                     writes=[ATp_b], sig=(t == nt - 1))
            P.copy("act", AT[s2][:, 0:nt, :], ATp[:, 0:nt, :], reads=[ATp_b], writes=[AT_b[s2]])

        def stage3(n):
            ch = chunks[n]
            hl, B, u0, nn, w, pr = ch["hl"], ch["B"], ch["u0"], ch["n"], ch["w"], ch["pr"]
            s2 = n % 2
            nt = nn // 128
            hs = slice(64 * hl, 64 * hl + 64)
            sl = ch["slot"]
            for t in range(nt):
                tg = u0 // 128 + t
                P.mm(oTp[hs, sl, :], vr[w][:, tg, hs], AT[s2][:, t, :],
                     ch["first"] and t == 0, ch["last"] and t == nt - 1,
                     reads=[v_b[w], AT_b[s2]], writes=[oTs_b[sl]], sig=(t == nt - 1))
            if ch["last"]:
                P.copy("act", oT[hs, pr, B * 128:(B + 1) * 128], oTp[hs, sl, :], reads=[oTs_b[sl]],
                       writes=[oT_b])

        bg = ctx.bg_items if (i == 0) else []
        bgcv = Converter(P, [_S(stg[0]), _S(stg[1])], [_S(sq[:].rearrange("p a b -> p (a b)"))])
        bgcv.b32 = xc_b
        bgcv.b16 = [sq_b]
        bgcv.engs = ("act",)

        for it in norm_items(0):
            it()
        for it in proj_items(0, 0):
            it()
        nxt_items = []
        nrm_items = []
        wo_q = []
        N = len(chunks)
        stage1(0)
        stage1s(0)
        stage1(1)
        stage1s(1)
        for n in range(N + 3):
            if n + 2 < N:
                stage1(n + 2)
            if 0 <= n - 3 < N:
                stage3(n - 3)
            if 0 <= n - 2 < N:
                stage2(n - 2)
            if n < N:
                stage1b(n)
            if n + 2 < N:
                stage1s(n + 2)
            if n < N:
                ch = chunks[n]
                s_, pr, kp = ch["s"], ch["pr"], ch["kp"]
                if kp == 4:
                    if pr + 1 < 8:
                        nxt_items = proj_items(pr + 1, (pr + 1) % 2)
                    elif s_ + 1 < NSEQ:
                        nxt_items = proj_items(0, 0)
                    if nxt_items:
                        nxt_items.pop(0)()
                if kp >= 4 and nxt_items:
                    nxt_items.pop(0)()
                if pr == 6 and kp == 20 and s_ + 1 < NSEQ:
                    nrm_items = norm_items(s_ + 1)
                if pr == 6 and kp >= 20 and kp % 4 == 0 and nrm_items:
                    nrm_items.pop(0)()
                if pr == 0 and kp == 2 and s_ > 0:
                    allw = wo_items(s_ - 1)
                    wo_q = allw[0:8] + allw[24:32] + allw[8:24]
                if pr == 0 and wo_q:
                    cnt = 8 if kp in (2, 3) else (2 if kp >= 4 else 0)
                    for _ in range(cnt):
                        if wo_q:
                            wo_q.pop(0)()
                if bg and pr < 6 and kp % 8 == 7:
                    bg.pop(0)(bgcv)
        assert not nxt_items and not nrm_items and not wo_q
        for it in wo_items(NSEQ - 1):
            it()
        while bg:
            bg.pop(0)(bgcv)
        P.run()


def phase_final(ctx):
    nc = ctx.nc
    with ExitStack() as st:
        sb = lambda n, s, d: st.enter_context(nc.sbuf_tensor(_uid(ctx, n), s, d))
        ps = lambda n, s, d: st.enter_context(nc.psum_tensor(_uid(ctx, n), s, d))
        P = Prog(ctx)
        xn = [sb(f"xn{k}", [128, 8, 512], F32) for k in range(3)]
        xn_b = [Buf() for _ in range(3)]
        sq = sb("sq", [128, 8, 512], BF16)
        sq_b = Buf()
        rstd = sb("rstd", [128, 512], F32)
        rstd_b = Buf()
        ostg = [sb(f"ostg{k}", [128, 1024], F32) for k in range(3)]
        ostg_b = [Buf() for _ in range(3)]
        pst = ps("pst", [128, 512], F32)
        pst_b = Buf()
        pt = [ps(f"pt{k}", [128, 1024], F32) for k in range(2)]
        pt_b = [Buf() for _ in range(2)]
        NT = ctx.TOK // 512
        oi = 0

        def load(n):
            b = n % 3
            P.dma("sp", xn[b][:], ctx.xT[:, :, n * 512:(n + 1) * 512], writes=[xn_b[b]])

        def front(n):
            b = n % 3
            emit_norm_stats(P, ctx, xn[b], xn_b[b], sq, sq_b, pst, pst_b, rstd, rstd_b)
            for k in range(8):
                P.stt("dve", xn[b][:, k, :], xn[b][:, k, :], ctx.gT[:, 64 + k:65 + k], rstd[:], ALU.mult, ALU.mult,
                      reads=[xn_b[b], rstd_b], writes=[xn_b[b]])

        load(0)
        if NT > 1:
            load(1)
        front(0)
        for n in range(NT):
            b = n % 3
            if n + 2 < NT:
                load(n + 2)
            if n + 1 < NT:
                front(n + 1)
            for jj in range(4):
                p = oi % 2
                s3 = oi % 3
                oi += 1
                for k in range(8):
                    P.tr(pt[p][:, k * 128:(k + 1) * 128], xn[b][:, k, jj * 128:(jj + 1) * 128], ctx.identf[:],
                         reads=[xn_b[b]], writes=[pt_b[p]], sig=(k == 7))
                P.copy("act" if oi % 2 == 0 else "dve", ostg[s3][:], pt[p][:], reads=[pt_b[p]],
                       writes=[ostg_b[s3]])
                t0 = n * 512 + jj * 128
                P.dma("sp", ctx.d_out[t0:t0 + 128, :], ostg[s3][:], reads=[ostg_b[s3]])
        P.run()


DBG_N = 0
SUB_DVE_MAX = 384
N_DUMMY = 0
BLOCK_ORDER = [0, 15, 1, 14, 2, 13, 3, 12, 4, 11, 5, 10, 6, 9, 7, 8]


def build(tok=4 * SEQ, depth=DEPTH, debug_xT=False, debug_attn=False):
    nc = bass.Bass("TRN2", target_bir_lowering=False)
    ctx = Ctx()
    ctx.nc = nc
    ctx.TOK = tok
    ctx.depth = depth
    ctx.sb_layers = [i // 2 for i in range(depth) if i % 2 == 0]
    ctx.sgu_layers = [i // 2 for i in range(depth) if i % 2 == 1]
    ctx.mlp_layers = list(range(depth))
    ein = lambda n, s, d=F32: nc.dram_tensor(n, s, d, kind="ExternalInput").ap()
    ctx.d_x = ein("x", [tok, D])
    ctx.d_norm_mix = ein("norm_mix", [DEPTH, D])
    ctx.d_norm_mlp = ein("norm_mlp", [DEPTH, D])
    ctx.d_wqkv = ein("sb_wqkv", [2, D, 3 * D])
    ctx.d_wo = ein("sb_wo", [2, D, D])
    ctx.d_win = ein("sgu_win", [2, D, 2 * SGF])
    ctx.d_sgain = ein("sgu_gain", [2, SGF])
    ctx.d_sws = ein("sgu_ws", [2, 8, 128, 128])
    ctx.d_sbs = ein("sgu_bs", [2, 8, 128])
    ctx.d_wout = ein("sgu_wout", [2, SGF, D])
    ctx.d_w1 = ein("mlp_w1", [DEPTH, D, FF])
    ctx.d_w2 = ein("mlp_w2", [DEPTH, FF, D])
    ctx.d_final_norm = ein("final_norm", [D])
    ctx.d_identf = ein("c_identf", [128, 128])
    ctx.d_identb = ein("c_identb", [128, 128], BF16)
    ctx.d_onesb = ein("c_onesb", [128, 128], BF16)
    ctx.d_maskneg = ein("c_maskneg", [128, 128], BF16)
    ctx.d_tril = ein("c_tril", [128, 128])
    ctx.d_out = nc.dram_tensor("out", [tok, D], F32, kind="ExternalOutput").ap()
    ctx.xT = nc.dram_tensor("xT_scr", [128, 8, tok], F32, kind=("ExternalOutput" if debug_xT else "Internal")).ap()
    ctx.wqkvp = [nc.dram_tensor(f"wqkvp{j}", [128, 8, 3, 8, 128], BF16).ap() for j in range(2)]
    ctx.woutf = [nc.dram_tensor(f"woutf{j}", [128, 8, 16, 128], BF16).ap() for j in range(2)]
    ctx.w2f = [nc.dram_tensor(f"w2f{i}", [128, 8, 32, 128], BF16).ap() for i in range(DEPTH)]

    if debug_attn:
        eo = lambda n, s, d: nc.dram_tensor(n, s, d, kind="ExternalOutput").ap()
        ctx.dbg = dict(hrT=eo("d_hrT", [128, 8, 2048], BF16), qT=eo("d_qT", [128, 2048], BF16),
                       kT=eo("d_kT", [128, 2048], BF16), vr=eo("d_vr", [128, 16, 128], BF16),
                       G=eo("d_G", [128, 1024], F32), P=eo("d_P", [128, 2049], F32), A=eo("d_A", [128, 1024], BF16),
                       AT=eo("d_AT", [128, 8, 128], BF16), oT=eo("d_oT", [128, 8, 2048], BF16))
    with ExitStack() as gst:
        sbg = lambda n, s, d: gst.enter_context(nc.sbuf_tensor(n, s, d))
        ctx.identf = sbg("identf", [128, 128], F32)
        ctx.identb = sbg("identb", [128, 128], BF16)
        ctx.onesb = sbg("onesb", [128, 128], BF16)
        ctx.maskneg = sbg("maskneg", [128, 128], BF16)
        ctx.tril = sbg("tril", [128, 128], F32)
        ctx.zero1 = sbg("zero1", [128, 1], F32)
        ctx.epsc = sbg("epsc", [128, 1], F32)
        ctx.mhalf = sbg("mhalf", [128, 8], F32)
        ctx.gT = sbg("gT", [128, 72], F32)
        ctx.prog_sem = {e: Sem(gst.enter_context(nc.semaphore(f"prog_{e}"))) for e in ENGS}
        ctx.dma_sems = {
            "sp": [Sem(gst.enter_context(nc.semaphore(f"dsp{k}"))) for k in range(12)],
            "pool": [Sem(gst.enter_context(nc.semaphore(f"dpl{k}"))) for k in range(6)],
        }
        ctx.waited = {e: {} for e in ENGS}
        ctx.w1b = [nc.dram_tensor(f"w1b{i_}", [128, 8, 4096], BF16).ap() for i_ in range(DEPTH)]
        ctx.winb = [nc.dram_tensor(f"winb{j_}", [128, 8, 4096], BF16).ap() for j_ in range(2)]
        ident = lambda a: a

        def it_wqkv(j_):
            out = []
            for kc in range(8):
                def f(cv, kc=kc):
                    dst = [(ctx.wqkvp[j_][:, :, w_, kc, :],
                            (lambda a, w_=w_: a[:, w_ * 1024:(w_ + 1) * 1024].rearrange("p (q c) -> p q c", q=8)))
                           for w_ in range(3)]
                    cv.go(ctx.d_wqkv[j_, kc * 128:(kc + 1) * 128, :], [128, 3072],
                          gain=ctx.gT[:, 2 * j_ * 8 + kc:2 * j_ * 8 + kc + 1], dst_dram=dst)
                out.append(f)
            return out

        def it_fmaj(src_t, dst_t, nrow4):
            out = []
            for c4 in range(nrow4):
                def f(cv, c4=c4):
                    src = src_t[c4 * 512:(c4 + 1) * 512, :].rearrange("(c p) n -> p c n", p=128)
                    dst = [(dst_t[:, :, c4 * 4 + c_, :],
                            (lambda a, c_=c_: a[:, c_ * 1024:(c_ + 1) * 1024].rearrange("p (f n) -> p f n", f=8)))
                           for c_ in range(4)]
                    cv.go(src, [128, 4, 1024], dst_dram=dst)
                out.append(f)
            return out

        def it_rowmaj(src_t, dst_t, gcol0):
            out = []
            for kc in range(8):
                def f(cv, kc=kc):
                    cv.go(src_t[kc * 128:(kc + 1) * 128, :], [128, 4096],
                          gain=ctx.gT[:, gcol0 + kc:gcol0 + kc + 1], dst_dram=[(dst_t[:, kc, :], ident)])
                out.append(f)
            return out

        ctx.prep_items = it_wqkv(0)
        bg = []
        for i_ in range(depth):
            if i_ % 2 == 1:
                j_ = i_ // 2
                bg += it_rowmaj(ctx.d_win[j_], ctx.winb[j_], i_ * 8)
                bg += it_fmaj(ctx.d_wout[j_], ctx.woutf[j_], 4)
            elif i_ > 0:
                bg += it_wqkv(i_ // 2)
            bg += it_rowmaj(ctx.d_w1[i_], ctx.w1b[i_], 32 + i_ * 8)
            bg += it_fmaj(ctx.d_w2[i_], ctx.w2f[i_], 8)
        ctx.bg_items = bg
        phase_prep(ctx)
        for i in range(depth):
            if i % 2 == 0:
                phase_attn(ctx, i, i // 2)
            else:
                phase_sgu(ctx, i, i // 2)
            phase_mlp(ctx, i)
        phase_final(ctx)
    return nc


def host_consts():
    ii = np.arange(128)[:, None]
    rr = np.arange(128)[None, :]
    return {
        "c_identf": np.eye(128, dtype=np.float32),
        "c_identb": np.eye(128).astype(ml_dtypes.bfloat16),
        "c_onesb": np.ones((128, 128)).astype(ml_dtypes.bfloat16),
        "c_maskneg": np.where(rr <= ii, -30000.0, 0.0).astype(ml_dtypes.bfloat16),
        "c_tril": (rr <= ii).astype(np.float32),
    }


WEIGHT_KEYS = ("norm_mix", "norm_mlp", "sb_wqkv", "sb_wo", "sgu_win", "sgu_gain", "sgu_ws", "sgu_bs",
               "sgu_wout", "mlp_w1", "mlp_w2", "final_norm")


def kernel(**inputs):
    x = np.ascontiguousarray(np.asarray(inputs["x"], dtype=np.float32))
    B = x.shape[0]
    per = B // NCORES
    shared = {k: np.ascontiguousarray(np.asarray(inputs[k], dtype=np.float32)) for k in WEIGHT_KEYS}
    shared.update(host_consts())
    nc = build(tok=per * SEQ)
    in_maps = []
    for c in range(NCORES):
        m = dict(shared)
        m["x"] = x[c * per:(c + 1) * per].reshape(per * SEQ, D)
        in_maps.append(m)
    res = run_bass_kernel_spmd(nc, in_maps, core_ids=list(range(NCORES)))
    outs = [np.asarray(r["out"]).reshape(per, SEQ, D) for r in res.results]
    return np.concatenate(outs, axis=0).astype(np.float32)
```

```python
import numpy as np
import ml_dtypes
from contextlib import ExitStack
import concourse.bass as bass
import concourse.mybir as mybir
from concourse.bass_utils import run_bass_kernel_spmd

F32 = mybir.dt.float32
BF16 = mybir.dt.bfloat16
AF = mybir.ActivationFunctionType
ALU = mybir.AluOpType

NCORES = 8
D = 1024
SEQ = 2048
DEPTH = 4
FF = 4096
SGF = 2048
EPS = 1e-6
ENGS = ("sp", "act", "dve", "pool", "pe")


class Sem:
    def __init__(self, h):
        self.h = h
        self.n = 0


class Buf:
    __slots__ = ("w", "r")

    def __init__(self):
        self.w = None
        self.r = []


class Ctx:
    pass


class Prog:
    def __init__(self, ctx):
        self.ctx = ctx
        self.q = {e: [] for e in ENGS}
        self.waited = ctx.waited
        self.pend = {e: ([], []) for e in ENGS}
        self.dma_rr = {e: 0 for e in ENGS}

    def _waits(self, eng, toks):
        best = {}
        w = self.waited[eng]
        pe_self = self.ctx.prog_sem["pe"] if eng == "pe" else None
        for t in toks:
            if t is None:
                continue
            s, v = t
            if s is pe_self:
                continue
            if v <= w.get(s, 0):
                continue
            if v > best.get(s, 0):
                best[s] = v
        out = []
        for s, v in best.items():
            w[s] = v
            out.append((s, v))
        return out

    def op(self, eng, fn, reads=(), writes=(), sig=True):
        toks = []
        for b in reads:
            toks.append(b.w)
        for b in writes:
            toks.append(b.w)
            toks.extend(b.r)
        waits = self._waits(eng, toks)
        pr, pw = self.pend[eng]
        pr.extend(reads)
        pw.extend(writes)
        inc = None
        tok = None
        if sig:
            s = self.ctx.prog_sem[eng]
            s.n += 1
            tok = (s, s.n)
            inc = (s.h, 1)
            for b in pr:
                b.r.append(tok)
            for b in pw:
                b.w = tok
                b.r = []
            pr.clear()
            pw.clear()
        self.q[eng].append((waits, fn, inc))
        return tok

    def dma(self, eng, out, in_, reads=(), writes=()):
        sems = self.ctx.dma_sems[eng]
        k = self.dma_rr[eng]
        self.dma_rr[eng] = (k + 1) % len(sems)
        s = sems[k]
        toks = [(s, s.n)] if s.n > 0 else []
        for b in reads:
            toks.append(b.w)
        for b in writes:
            toks.append(b.w)
            toks.extend(b.r)
        waits = self._waits(eng, toks)
        s.n += 16
        tok = (s, s.n)
        for b in reads:
            b.r.append(tok)
        for b in writes:
            b.w = tok
            b.r = []
        self.q[eng].append((waits, (lambda e, o=out, i=in_: e.dma_start(out=o, in_=i)), (s.h, 16)))
        return tok

    def fence(self, buf):
        toks = [buf.w] + list(buf.r)
        for e in ENGS:
            self.q[e].append((self._waits(e, toks), None, None))

    def mm(self, out, lhsT, rhs, start, stop, reads=(), writes=(), sig=False):
        return self.op("pe", lambda e: e.matmul(out, lhsT=lhsT, rhs=rhs, start=start, stop=stop),
                       reads, writes, sig)

    def tr(self, out, in_, ident, reads=(), writes=(), sig=False):
        return self.op("pe", lambda e: e.transpose(out=out, in_=in_, identity=ident), reads, writes, sig)

    def act(self, out, in_, func, reads=(), writes=(), **kw):
        return self.op("act", lambda e: e.activation(out=out, in_=in_, func=func, **kw), reads, writes)

    def tt(self, eng, out, in0, in1, op, reads=(), writes=()):
        return self.op(eng, lambda e: e.tensor_tensor(out=out, in0=in0, in1=in1, op=op), reads, writes)

    def ts(self, eng, out, in0, s1, s2, op0, op1=None, reads=(), writes=()):
        if op1 is None:
            return self.op(eng, lambda e: e.tensor_scalar(out=out, in0=in0, scalar1=s1, scalar2=None, op0=op0),
                           reads, writes)
        return self.op(eng, lambda e: e.tensor_scalar(out=out, in0=in0, scalar1=s1, scalar2=s2, op0=op0, op1=op1),
                       reads, writes)

    def stt(self, eng, out, in0, scalar, in1, op0, op1, reads=(), writes=()):
        return self.op(eng, lambda e: e.scalar_tensor_tensor(out=out, in0=in0, scalar=scalar, in1=in1,
                                                              op0=op0, op1=op1), reads, writes)

    def copy(self, eng, out, in_, reads=(), writes=()):
        if eng == "act":
            return self.op("act", lambda e: e.activation(out=out, in_=in_, func=AF.Copy), reads, writes)
        return self.op(eng, lambda e: e.tensor_copy(out=out, in_=in_), reads, writes)

    def run(self):
        ctx = self.ctx
        nc = ctx.nc
        finals = []
        for e in ENGS:
            s = ctx.prog_sem[e]
            if s.n > 0:
                finals.append((s, s.n))
            for d in ctx.dma_sems.get(e, []):
                if d.n > 0:
                    finals.append((d, d.n))
        for e in ENGS:
            self.q[e].append((self._waits(e, finals), None, None))
        q = self.q

        def mk(eng):
            def body(e):
                for waits, fn, inc in q[eng]:
                    for s, v in waits:
                        e.wait_ge(s.h, v)
                    if fn is not None:
                        ins = fn(e)
                        if inc is not None:
                            ins.then_inc(inc[0], inc[1])
            return body

        with nc.Block() as block:
            block.sync(mk("sp"))
            block.scalar(mk("act"))
            block.vector(mk("dve"))
            block.gpsimd(mk("pool"))
            block.tensor(mk("pe"))


def _uid(ctx, n):
    ctx.uid = getattr(ctx, "uid", 0) + 1
    return f"t{ctx.uid}_{n}"


def _rs(start, n):
    stop = start - n
    return slice(start, stop if stop >= 0 else None, -1)


class Converter:
    def __init__(self, P, stg32, stg16=None):
        self.P = P
        self.stg32 = stg32
        self.b32 = [Buf() for _ in stg32]
        self.stg16 = stg16
        self.b16 = [Buf() for _ in stg16] if stg16 else None
        self.i = 0
        self.engs = ("act", "dve")

    def go(self, src, shape, gain=None, dst_sb=None, dst_buf=None, dst_dram=None):
        P = self.P
        k = self.i
        self.i += 1
        s = k % len(self.stg32)
        n = int(np.prod(shape[1:]))
        s32 = self.stg32[s][:, 0:n]
        if len(shape) == 3:
            s32v = s32.rearrange("p (a b) -> p a b", b=shape[2])
        else:
            s32v = s32
        P.dma("sp", s32v, src, writes=[self.b32[s]])
        eng = self.engs[k % len(self.engs)]
        if dst_sb is not None:
            out = dst_sb
            wr = [dst_buf]
        else:
            s2 = k % len(self.stg16)
            o16 = self.stg16[s2][:, 0:n]
            out = o16.rearrange("p (a b) -> p a b", b=shape[2]) if len(shape) == 3 else o16
            wr = [self.b16[s2]]
        if gain is not None:
            if eng == "act":
                P.act(out, s32v, AF.Copy, reads=[self.b32[s]], writes=wr, scale=gain)
            else:
                P.ts(eng, out, s32v, gain, None, ALU.mult, reads=[self.b32[s]], writes=wr)
        else:
            P.copy(eng, out, s32v, reads=[self.b32[s]], writes=wr)
        if dst_sb is None:
            for dd, vv in dst_dram:
                P.dma("sp", dd, vv(self.stg16[s2][:, 0:n]), reads=[self.b16[s2]])


def emit_norm_sq_piece(P, xn, xn_b, sq, sq_b, k):
    P.act(sq[:, k, :], xn[:, k, :], AF.Square, reads=[xn_b], writes=[sq_b])


def emit_norm_stats(P, ctx, xn, xn_b, sq, sq_b, pst, pst_b, rstd, rstd_b, do_square=True):
    if do_square:
        P.act(sq[:].rearrange("p a b -> p (a b)"), xn[:].rearrange("p a b -> p (a b)"), AF.Square,
              reads=[xn_b], writes=[sq_b])
    for k in range(8):
        P.mm(pst[:], ctx.onesb[:], sq[:, k, :], k == 0, k == 7, reads=[sq_b], writes=[pst_b], sig=(k == 7))
    P.act(rstd[:], pst[:], AF.Sqrt, reads=[pst_b], writes=[rstd_b], bias=ctx.epsc[:, 0:1], scale=1.0 / D)
    P.op("dve", lambda e: e.reciprocal(out=rstd[:], in_=rstd[:]), reads=[rstd_b], writes=[rstd_b])


def phase_prep(ctx):
    nc = ctx.nc
    with ExitStack() as st:
        sb = lambda n, s, d: st.enter_context(nc.sbuf_tensor(_uid(ctx, n), s, d))
        P = Prog(ctx)
        cb = Buf()
        P.dma("sp", ctx.identf[:], ctx.d_identf[:, :], writes=[cb])
        P.dma("sp", ctx.identb[:], ctx.d_identb[:, :], writes=[cb])
        P.dma("sp", ctx.onesb[:], ctx.d_onesb[:, :], writes=[cb])
        P.dma("sp", ctx.maskneg[:], ctx.d_maskneg[:, :], writes=[cb])
        P.dma("sp", ctx.tril[:], ctx.d_tril[:, :], writes=[cb])
        P.op("dve", lambda e: e.memset(ctx.zero1[:], 0.0), writes=[cb])
        P.op("dve", lambda e: e.memset(ctx.epsc[:], EPS), writes=[cb])
        P.op("dve", lambda e: e.memset(ctx.mhalf[:], -0.5), writes=[cb])
        g72 = sb("g72", [72, 128], F32)
        gb = Buf()
        P.dma("sp", g72[0:32, :], ctx.d_norm_mix.rearrange("l (k p) -> (l k) p", p=128), writes=[gb])
        P.dma("sp", g72[32:64, :], ctx.d_norm_mlp.rearrange("l (k p) -> (l k) p", p=128), writes=[gb])
        P.dma("sp", g72[64:72, :], ctx.d_final_norm.rearrange("(k p) -> k p", p=128), writes=[gb])
        pg = st.enter_context(nc.psum_tensor(_uid(ctx, "pg"), [128, 72], F32))
        pgb = Buf()
        P.tr(pg[:], g72[:], ctx.identf[0:72, 0:72], reads=[gb, cb], writes=[pgb], sig=True)
        P.copy("dve", ctx.gT[:], pg[:], reads=[pgb], writes=[cb])
        P.fence(cb)

        stg32 = [sb(f"pstg32_{k}", [128, 4096], F32) for k in range(3)]
        stg16 = [sb(f"pstg16_{k}", [128, 4096], BF16) for k in range(3)]
        cv = Converter(P, stg32, stg16)
        cv.b32 = [Buf() for _ in stg32]
        for it in ctx.prep_items:
            it(cv)

        xin = [sb(f"xin{k}", [128, 1024], F32) for k in range(6)]
        xin_b = [Buf() for _ in range(6)]
        xts = [sb(f"xts{k}", [128, 8, 512], F32) for k in range(2)]
        xts_b = [Buf() for _ in range(2)]
        ptx = [st.enter_context(nc.psum_tensor(_uid(ctx, f"ptx{k}"), [128, 8, 128], F32)) for k in range(2)]
        ptx_b = [Buf() for _ in range(2)]
        ntt = ctx.TOK // 128
        def xload(t):
            P.dma("sp", xin[t % 6][:], ctx.d_x[t * 128:(t + 1) * 128, :], writes=[xin_b[t % 6]])

        for t in range(min(5, ntt)):
            xload(t)
        for t in range(ntt):
            s = t % 6
            if t + 5 < ntt:
                xload(t + 5)
            pp = t % 2
            for k in range(8):
                P.tr(ptx[pp][:, k, :], xin[s][:, k * 128:(k + 1) * 128], ctx.identf[:],
                     reads=[xin_b[s], cb], writes=[ptx_b[pp]], sig=(k == 7))
            big = (t // 4) % 2
            P.copy("act" if t % 2 == 0 else "dve", xts[big][:, :, (t % 4) * 128:(t % 4 + 1) * 128], ptx[pp][:],
                   reads=[ptx_b[pp]], writes=[xts_b[big]])
            if t % 4 == 3:
                t0 = (t // 4) * 512
                P.dma("sp", ctx.xT[:, :, t0:t0 + 512], xts[big][:], reads=[xts_b[big]])
        P.run()


def phase_mlp(ctx, i):
    nc = ctx.nc
    with ExitStack() as st:
        sb = lambda n, s, d: st.enter_context(nc.sbuf_tensor(_uid(ctx, n), s, d))
        ps = lambda n, s, d: st.enter_context(nc.psum_tensor(_uid(ctx, n), s, d))
        P = Prog(ctx)
        w1 = sb("w1", [128, 8, 4096], BF16)
        w1_b = Buf()
        w2s = [sb(f"w2s{k}", [128, 32, 128], BF16) for k in range(3)]
        w2s_b = [Buf() for _ in range(3)]
        xn = sb("xn", [128, 8, 512], F32)
        xn_b = Buf()
        sq = sb("sq", [128, 8, 512], BF16)
        sq_b = Buf()
        rstd = sb("rstd", [128, 512], F32)
        rstd_b = Buf()
        hT = [sb(f"hT{k}", [128, 8, 512], BF16) for k in range(2)]
        hT_b = [Buf() for _ in range(2)]
        hid = sb("hid", [128, 32, 512], BF16)
        hid_b = Buf()
        rt = [sb(f"rt{k}", [128, 512], BF16) for k in range(2)]
        rt_b = [Buf() for _ in range(2)]
        xres = [sb(f"xres{k}", [128, 512], F32) for k in range(3)]
        xres_b = [Buf() for _ in range(3)]
        pst = ps("pst", [128, 512], F32)
        pst_b = Buf()
        ph = [ps(f"ph{k}", [128, 512], F32) for k in range(2)]
        ph_b = [Buf() for _ in range(2)]
        po = [ps(f"po{k}", [128, 512], F32) for k in range(2)]
        po_b = [Buf() for _ in range(2)]

        bg = ctx.bg_items if i == 0 else []
        bgcv = None
        if bg:
            bstg32 = sb("bstg32", [128, 4096], F32)
            bstg16 = sb("bstg16", [128, 4096], BF16)
            bgcv = Converter(P, [bstg32], [bstg16])
            bgcv.engs = ("dve",)
        w1_bs = [Buf() for _ in range(4)]
        P.dma("sp", xn[:], ctx.xT[:, :, 0:512], writes=[xn_b])
        for q4 in range(4):
            P.dma("sp", w1[:, 2 * q4:2 * q4 + 2, :], ctx.w1b[i][:, 2 * q4:2 * q4 + 2, :], writes=[w1_bs[q4]])

        NT = ctx.TOK // 512

        def norm_front(n):
            P.dma("sp", xn[:], ctx.xT[:, :, n * 512:(n + 1) * 512], writes=[xn_b])

        def norm_back(n, do_square=True):
            emit_norm_stats(P, ctx, xn, xn_b, sq, sq_b, pst, pst_b, rstd, rstd_b, do_square=do_square)
            for k in range(8):
                P.tt("dve", hT[n % 2][:, k, :], xn[:, k, :], rstd[:], ALU.mult,
                     reads=[xn_b, rstd_b], writes=[hT_b[n % 2]])

        norm_back(0)
        gidx = 0
        for n in range(NT):
            tok0 = n * 512
            if n + 1 < NT:
                norm_front(n + 1)
            h = hT[n % 2]
            for c in range(32):
                p = c % 2
                if bg and c % 8 == 0:
                    bg.pop(0)(bgcv)
                for k in range(8):
                    P.mm(ph[p][:], w1[:, k, c * 128:(c + 1) * 128], h[:, k, :], k == 0, k == 7,
                         reads=w1_bs + [hT_b[n % 2]], writes=[ph_b[p]], sig=(k == 7))
                P.act(rt[p][:], ph[p][:], AF.Relu, reads=[ph_b[p]], writes=[rt_b[p]])
                P.tt("dve", hid[:, c, :], rt[p][:], rt[p][:], ALU.mult, reads=[rt_b[p]], writes=[hid_b])
                if n + 1 < NT and 8 <= c < 24 and c % 2 == 0:
                    emit_norm_sq_piece(P, xn, xn_b, sq, sq_b, (c - 8) // 2)
            if n + 1 < NT:
                norm_back(n + 1, do_square=False)
            for f in range(8):
                s = gidx % 3
                gidx += 1
                P.dma("sp", w2s[s][:], ctx.w2f[i][:, f, :, :], writes=[w2s_b[s]])
                P.dma("sp", xres[s][:], ctx.xT[:, f, tok0:tok0 + 512], writes=[xres_b[s]])
                p = f % 2
                for c in range(32):
                    P.mm(po[p][:], w2s[s][:, c, :], hid[:, c, :], c == 0, c == 31,
                         reads=[w2s_b[s], hid_b], writes=[po_b[p]], sig=(c == 31))
                P.tt("dve", xres[s][:], xres[s][:], po[p][:], ALU.add, reads=[po_b[p], xres_b[s]],
                     writes=[xres_b[s]])
                P.dma("pool", ctx.xT[:, f, tok0:tok0 + 512], xres[s][:], reads=[xres_b[s]])
        while bg:
            bg.pop(0)(bgcv)
        P.run()


def phase_sgu(ctx, i, j):
    nc = ctx.nc
    with ExitStack() as st:
        sb = lambda n, s, d: st.enter_context(nc.sbuf_tensor(_uid(ctx, n), s, d))
        ps = lambda n, s, d: st.enter_context(nc.psum_tensor(_uid(ctx, n), s, d))
        P = Prog(ctx)
        win = sb("win", [128, 8, 4096], BF16)
        win_b = Buf()
        wouts = [sb(f"wouts{k}", [128, 16, 128], BF16) for k in range(3)]
        wouts_b = [Buf() for _ in range(3)]
        wsT = sb("wsT", [128, 8, 128], BF16)
        gain_bc = sb("gain_bc", [128, 2048], F32)
        b_bc = sb("b_bc", [128, 16, 128], F32)
        cst_b = Buf()
        xn = sb("xn", [128, 8, 512], F32)
        xn_b = Buf()
        sq = sb("sq", [128, 8, 512], BF16)
        sq_b = Buf()
        rstd = sb("rstd", [128, 512], F32)
        rstd_b = Buf()
        hT = [sb(f"hT{k}", [128, 8, 512], BF16) for k in range(2)]
        hT_b = [Buf() for _ in range(2)]
        uT = sb("uT", [128, 16, 512], BF16)
        uT_b = Buf()
        vgs = [sb(f"vg{k}", [128, 2048], F32) for k in range(2)]
        vgs_b = [Buf() for _ in range(2)]
        vn = [sb(f"vn{k}", [128, 2048], BF16) for k in range(2)]
        vn_b = [Buf() for _ in range(2)]
        ssq = [sb(f"ssq{k}", [128, 1], F32) for k in range(2)]
        ssq_b = [Buf() for _ in range(2)]
        yT = sb("yT", [128, 16, 512], BF16)
        yT_b = Buf()
        xres = [sb(f"xres{k}", [128, 512], F32) for k in range(3)]
        xres_b = [Buf() for _ in range(3)]
        stg32 = [sb(f"stg32_{k}", [128, 128], F32) for k in range(2)]
        stg_b = [Buf() for _ in range(2)]
        pst = ps("pst", [128, 512], F32)
        pst_b = Buf()
        pu = [ps(f"pu{k}", [128, 512], F32) for k in range(2)]
        pu_b = [Buf() for _ in range(2)]
        pm = [ps(f"pm{k}", [128, 4, 128], F32) for k in range(2)]
        pm_b = [Buf() for _ in range(2)]
        po = [ps(f"po{k}", [128, 512], F32) for k in range(2)]
        po_b = [Buf() for _ in range(2)]

        P.dma("sp", gain_bc[:], ctx.d_sgain[j, :].partition_broadcast(128), writes=[cst_b])
        bsrc = ctx.d_sbs[j, :, :].rearrange("g t -> (g t)").partition_broadcast(128).rearrange("p (g t) -> p g t", t=128)
        P.dma("sp", b_bc[:, 0:16:2, :], bsrc, writes=[cst_b])
        P.dma("sp", b_bc[:, 1:16:2, :], bsrc, writes=[cst_b])
        for g in range(8):
            s32 = stg32[g % 2]
            sbuf_b = stg_b[g % 2]
            P.dma("sp", s32[:, 0:128], ctx.d_sws[j, g, :, :], writes=[sbuf_b])
            P.tt("dve", s32[:, 0:128], s32[:, 0:128], ctx.tril[:], ALU.mult, reads=[sbuf_b], writes=[sbuf_b])
            P.tr(pu[g % 2][:, 0:128], s32[:, 0:128], ctx.identf[:], reads=[sbuf_b], writes=[pu_b[g % 2]], sig=True)
            P.copy("act", wsT[:, g, :], pu[g % 2][:, 0:128], reads=[pu_b[g % 2]], writes=[cst_b])
        win_bs = [Buf() for _ in range(4)]
        for q4 in range(4):
            P.dma("sp", win[:, 2 * q4:2 * q4 + 2, :], ctx.winb[j][:, 2 * q4:2 * q4 + 2, :], writes=[win_bs[q4]])

        NT = ctx.TOK // 512

        def norm_front(n):
            P.dma("sp", xn[:], ctx.xT[:, :, n * 512:(n + 1) * 512], writes=[xn_b])

        def norm_back(n, do_square=True):
            emit_norm_stats(P, ctx, xn, xn_b, sq, sq_b, pst, pst_b, rstd, rstd_b, do_square=do_square)
            for k in range(8):
                P.tt("dve", hT[n % 2][:, k, :], xn[:, k, :], rstd[:], ALU.mult,
                     reads=[xn_b, rstd_b], writes=[hT_b[n % 2]])

        norm_front(0)
        norm_back(0)
        gidx = 0
        pui = 0
        pmi = 0
        for n in range(NT):
            tok0 = n * 512
            if n + 1 < NT:
                norm_front(n + 1)
            h = hT[n % 2]
            hb = hT_b[n % 2]
            for c in range(16):
                p = pui % 2
                pui += 1
                for k in range(8):
                    P.mm(pu[p][:], win[:, k, c * 128:(c + 1) * 128], h[:, k, :], k == 0, k == 7,
                         reads=win_bs + [hb], writes=[pu_b[p]], sig=(k == 7))
                P.act(uT[:, c, :], pu[p][:], AF.Gelu_apprx_tanh, reads=[pu_b[p]], writes=[uT_b])
                if n + 1 < NT and 4 <= c < 12:
                    emit_norm_sq_piece(P, xn, xn_b, sq, sq_b, c - 4)

            def v_stage(jj):
                nonlocal pui
                vg, vg_b = vgs[jj % 2], vgs_b[jj % 2]
                for q4 in range(4):
                    p = pui % 2
                    pui += 1
                    for k in range(8):
                        P.mm(pu[p][:], h[:, k, jj * 128:(jj + 1) * 128],
                             win[:, k, 2048 + q4 * 512:2048 + (q4 + 1) * 512], k == 0, k == 7,
                             reads=win_bs + [hb], writes=[pu_b[p]], sig=(k == 7))
                    P.act(vg[:, q4 * 512:(q4 + 1) * 512], pu[p][:], AF.Gelu_apprx_tanh, reads=[pu_b[p]],
                          writes=[vg_b])

            def v_chain(jj):
                v2 = jj % 2
                vg, vg_b = vgs[jj % 2], vgs_b[jj % 2]
                P.op("dve", lambda e: e.memset(ssq[v2][:], 0.0), writes=[ssq_b[v2]])
                P.op("dve", lambda e: e.scalar_tensor_tensor(out=vn[v2][:], in0=vg[:], scalar=1.0, in1=vg[:],
                                                             op0=ALU.mult, op1=ALU.mult, accum_out=ssq[v2][:]),
                     reads=[vg_b, ssq_b[v2]], writes=[vn_b[v2], ssq_b[v2]])
                P.ts("dve", ssq[v2][:], ssq[v2][:], 1.0 / SGF, EPS, ALU.mult, ALU.add, reads=[ssq_b[v2]],
                     writes=[ssq_b[v2]])
                P.tt("pool", ssq[v2][:], ssq[v2][:], ctx.mhalf[:, 0:1], ALU.pow, reads=[ssq_b[v2]],
                     writes=[ssq_b[v2]])
                P.stt("dve", vn[v2][:], vg[:], ssq[v2][:, 0:1], gain_bc[:], ALU.mult, ALU.mult,
                      reads=[vg_b, ssq_b[v2], cst_b], writes=[vn_b[v2]])

            def mix_stage(jj):
                nonlocal pmi
                v2 = jj % 2
                for qq in range(4):
                    p = pmi % 2
                    pmi += 1
                    for cc in range(4):
                        c = qq * 4 + cc
                        g = c // 2
                        P.mm(pm[p][:, cc, :], vn[v2][:, c * 128:(c + 1) * 128], wsT[:, g, :], True, True,
                             reads=[vn_b[v2], cst_b], writes=[pm_b[p]], sig=(cc == 3))
                    yv = yT[:, qq * 4:(qq + 1) * 4, jj * 128:(jj + 1) * 128]
                    P.tt("dve", yv, pm[p][:], b_bc[:, qq * 4:(qq + 1) * 4, :], ALU.add,
                         reads=[pm_b[p], cst_b], writes=[yT_b])
                    P.tt("dve", yv, yv, uT[:, qq * 4:(qq + 1) * 4, jj * 128:(jj + 1) * 128], ALU.mult,
                         reads=[yT_b, uT_b], writes=[yT_b])

            v_stage(0)
            v_chain(0)
            for jj in range(1, 4):
                v_stage(jj)
                mix_stage(jj - 1)
                v_chain(jj)
            mix_stage(3)
            if n + 1 < NT:
                norm_back(n + 1, do_square=False)
            for f in range(8):
                s = gidx % 3
                gidx += 1
                P.dma("sp", wouts[s][:], ctx.woutf[j][:, f, :, :], writes=[wouts_b[s]])
                P.dma("sp", xres[s][:], ctx.xT[:, f, tok0:tok0 + 512], writes=[xres_b[s]])
                p = f % 2
                for c in range(16):
                    P.mm(po[p][:], wouts[s][:, c, :], yT[:, c, :], c == 0, c == 15,
                         reads=[wouts_b[s], yT_b], writes=[po_b[p]], sig=(c == 15))
                P.tt("dve", xres[s][:], xres[s][:], po[p][:], ALU.add, reads=[po_b[p], xres_b[s]],
                     writes=[xres_b[s]])
                P.dma("pool", ctx.xT[:, f, tok0:tok0 + 512], xres[s][:], reads=[xres_b[s]])
        P.run()


def phase_attn(ctx, i, j):
    nc = ctx.nc
    with ExitStack() as st:
        sb = lambda n, s, d: st.enter_context(nc.sbuf_tensor(_uid(ctx, n), s, d))
        ps = lambda n, s, d: st.enter_context(nc.psum_tensor(_uid(ctx, n), s, d))
        P = Prog(ctx)
        wo = sb("wo", [128, 8, 1024], BF16)
        wo_b = Buf()
        wp = [sb(f"wp{k}", [128, 3, 8, 128], BF16) for k in range(2)]
        wp_b = [Buf() for _ in range(2)]
        hrT = sb("hrT", [128, 8, 2048], BF16)
        hrT_b = Buf()
        qT = [sb(f"qT{k}", [128, 2048], BF16) for k in range(2)]
        kT = [sb(f"kT{k}", [128, 2048], BF16) for k in range(2)]
        vr = [sb(f"vr{k}", [128, 16, 128], BF16) for k in range(2)]
        q_b = [Buf() for _ in range(2)]
        k_b = [Buf() for _ in range(2)]
        v_b = [Buf() for _ in range(2)]
        oT = sb("oT", [128, 8, 2048], BF16)
        oT_b = Buf()
        G = [sb(f"G{k}", [128, 1024], F32) for k in range(3)]
        G_b = [Buf() for _ in range(3)]
        Pb = [sb(f"Pb{k}", [128, 2049], F32) for k in range(2)] + [sb("Pb2", [128, 1025], F32)]
        Pb_b = [Buf() for _ in range(3)]
        Pbh_b = [Buf() for _ in range(3)]
        A = [sb(f"A{k}", [128, 1024], BF16) for k in range(2)]
        A_b = [Buf() for _ in range(2)]
        AT = [sb(f"AT{k}", [128, 8, 128], BF16) for k in range(2)]
        AT_b = [Buf() for _ in range(2)]
        xc = [sb(f"xc{k}", [128, 8, 512], F32) for k in range(2)]
        xc_b = [Buf() for _ in range(2)]
        sq = sb("sq", [128, 8, 512], BF16)
        sq_b = Buf()
        rstd = sb("rstd", [128, 512], F32)
        rstd_b = Buf()
        xres = [sb(f"xres{k}", [128, 512], F32) for k in range(3)]
        xres_b = [Buf() for _ in range(3)]
        S = [ps(f"S{k}", [128, 1024], F32) for k in range(2)]
        S_b = [Buf() for _ in range(2)]
        ATp = ps("ATp", [128, 8, 128], BF16)
        ATp_b = Buf()
        oTp = ps("oTp", [128, 4, 128], F32)
        oTp_b = Buf()
        oTs_b = [Buf() for _ in range(4)]
        pp = [ps(f"pp{k}", [128, 512], F32) for k in range(2)]
        pp_b = [Buf() for _ in range(2)]
        ppi = 0

        stg = [xc[0][:].rearrange("p a b -> p (a b)"), xc[1][:].rearrange("p a b -> p (a b)")]

        class _S:
            def __init__(self, ap):
                self.ap = ap

            def __getitem__(self, idx):
                return self.ap[idx]

        cv = Converter(P, [_S(stg[0]), _S(stg[1])])
        cv.b32 = xc_b
        for kc2 in range(2):
            src = ctx.d_wo[j, kc2 * 512:(kc2 + 1) * 512, :].rearrange("(c p) n -> p c n", p=128)
            cv.go(src, [128, 4, 1024], dst_sb=wo[:, kc2 * 4:(kc2 + 1) * 4, :], dst_buf=wo_b)
        for k in range(3):
            P.op("dve", lambda e, k=k: e.memset(Pb[k][:, 0:1], 1.0), writes=[Pb_b[k], Pbh_b[k]])

        NSEQ = ctx.TOK // SEQ
        xi = 0

        def norm_items(s_):
            base = s_ * SEQ
            items = []

            def mk(tc):
                def f():
                    nonlocal xi
                    b = xi % 2
                    xi += 1
                    P.dma("sp", xc[b][:], ctx.xT[:, :, base + tc * 512:base + (tc + 1) * 512], writes=[xc_b[b]])
                    emit_norm_stats(P, ctx, xc[b], xc_b[b], sq, sq_b, pp[0], pp_b[0], rstd, rstd_b)
                    for k in range(8):
                        P.tt("dve", hrT[:, k, _rs(SEQ - 1 - tc * 512, 512)], xc[b][:, k, :], rstd[:], ALU.mult,
                             reads=[xc_b[b], rstd_b], writes=[hrT_b])
                return f
            for tc in range(4):
                items.append(mk(tc))
            return items

        def proj_items(pr, w):
            items = []

            def ld():
                P.dma("sp", wp[w][:], ctx.wqkvp[j][:, pr, :, :, :], writes=[wp_b[w]])

            def qk_item(which, tcq):
                def f():
                    nonlocal ppi
                    dstT, dst_b = (qT[w], q_b[w]) if which == 0 else (kT[w], k_b[w])
                    p = ppi % 2
                    ppi += 1
                    for k in range(8):
                        P.mm(pp[p][:], wp[w][:, which, k, :], hrT[:, k, tcq * 512:(tcq + 1) * 512],
                             k == 0, k == 7, reads=[wp_b[w], hrT_b], writes=[pp_b[p]], sig=(k == 7))
                    P.copy("act", dstT[:, tcq * 512:(tcq + 1) * 512], pp[p][:], reads=[pp_b[p]],
                           writes=[dst_b])
                return f

            def v_item(t4):
                def f():
                    nonlocal ppi
                    p = ppi % 2
                    ppi += 1
                    for tt_ in range(4):
                        tile = t4 * 4 + tt_
                        for k in range(8):
                            P.mm(pp[p][:, tt_ * 128:(tt_ + 1) * 128], hrT[:, k, tile * 128:(tile + 1) * 128],
                                 wp[w][:, 2, k, :], k == 0, k == 7, reads=[wp_b[w], hrT_b],
                                 writes=[pp_b[p]], sig=(k == 7 and tt_ == 3))
                    P.copy("act", vr[w][:, t4 * 4:(t4 + 1) * 4, :],
                           pp[p][:].rearrange("p (a b) -> p a b", b=128), reads=[pp_b[p]], writes=[v_b[w]])
                return f

            items.append(ld)
            for which in range(2):
                for tcq in range(4):
                    items.append(qk_item(which, tcq))
            for t4 in range(4):
                items.append(v_item(t4))
            return items

        gi = 0

        def wo_items(s_):
            base = s_ * SEQ
            items = []

            def mk(tcr, f):
                def g():
                    nonlocal ppi, gi
                    t_fwd = base + SEQ - 512 * (tcr + 1)
                    p = ppi % 2
                    ppi += 1
                    s3 = gi % 3
                    gi += 1
                    P.dma("sp", xres[s3][:], ctx.xT[:, f, t_fwd:t_fwd + 512], writes=[xres_b[s3]])
                    for pr in range(8):
                        P.mm(pp[p][:], wo[:, pr, f * 128:(f + 1) * 128], oT[:, pr, tcr * 512:(tcr + 1) * 512],
                             pr == 0, pr == 7, reads=[wo_b, oT_b], writes=[pp_b[p]], sig=(pr == 7))
                    P.tt("dve", xres[s3][:], xres[s3][:], pp[p][:, ::-1], ALU.add,
                         reads=[pp_b[p], xres_b[s3]], writes=[xres_b[s3]])
                    P.dma("pool", ctx.xT[:, f, t_fwd:t_fwd + 512], xres[s3][:], reads=[xres_b[s3]])
                return g
            for tcr in range(4):
                for f in range(8):
                    items.append(mk(tcr, f))
            return items

        chunks = []
        blk = 0
        bigc = 0
        for s_ in range(NSEQ):
            for pr in range(8):
                kp = 0
                for grp in range(1):
                    for hl in range(2):
                        for B in BLOCK_ORDER:
                            u0 = B * 128
                            nk = SEQ - u0
                            off = 0
                            while off < nk:
                                n_c = min(1024, nk - off)
                                chunks.append(dict(s=s_, pr=pr, w=pr % 2, grp=grp, hl=hl, B=B, u0=u0 + off,
                                                   off=off, n=n_c, first=(off == 0), last=(off + n_c == nk),
                                                   pb=(2 if nk <= 1024 else bigc % 2), slot=blk % 4, kp=kp))
                                kp += 1
                                off += n_c
                            blk += 1
                            if nk > 1024:
                                bigc += 1

        def stage1(n):
            ch = chunks[n]
            hl, B, u0, nn, w = ch["hl"], ch["B"], ch["u0"], ch["n"], ch["w"]
            s2 = n % 2
            hs = slice(64 * hl, 64 * hl + 64)
            qsl = qT[w][hs, B * 128:(B + 1) * 128]
            Sx = S[s2]
            mms = []
            if ch["first"]:
                mms.append((Sx[:, 0:128], ctx.identb[:], ctx.maskneg[:], True, False))
                mms.append((Sx[:, 0:128], qsl, kT[w][hs, u0:u0 + 128], False, True))
                pieces = [(128, min(nn, 512)), (512, nn)]
            else:
                pieces = [(0, min(nn, 512)), (512, nn)]
            for (a, b_) in pieces:
                if b_ > a:
                    mms.append((Sx[:, a:b_], qsl, kT[w][hs, u0 + a:u0 + b_], True, True))
            for d_ in range(N_DUMMY):
                P.mm(Sx[:, 0:512], ctx.identb[:], wo[:, d_ % 8, 0:512], True, True, reads=[wo_b], writes=[S_b[s2]])
            for idx, (o_, l_, r_, st_, sp_) in enumerate(mms):
                P.mm(o_, l_, r_, st_, sp_, reads=[q_b[w], k_b[w]], writes=[S_b[s2]],
                     sig=(idx == len(mms) - 1))

        def stage1s(n):
            ch = chunks[n]
            nn = ch["n"]
            s2 = n % 2
            g3 = n % 3
            Sx = S[s2]
            P.act(G[g3][:, 0:nn], Sx[:, 0:nn], AF.Sigmoid, reads=[S_b[s2]], writes=[G_b[g3]], scale=-0.125)
            pb = ch["pb"]
            off = ch["off"]
            init = 1.0 if ch["first"] else Pb[pb][:, off:off + 1]
            P.op("dve", lambda e: e.tensor_tensor_scan(
                out=Pb[pb][:, 1 + off:1 + off + nn], data0=G[g3][:, 0:nn],
                data1=ctx.zero1[:, 0:1].to_broadcast([128, nn]), initial=init,
                op0=ALU.mult, op1=ALU.add),
                reads=([G_b[g3]] if ch["first"] else [G_b[g3], Pb_b[pb]]),
                writes=([Pb_b[pb]] if ch["first"] else [Pbh_b[pb]]))

        def stage1b(n):
            ch = chunks[n]
            nn, pb, off = ch["n"], ch["pb"], ch["off"]
            s2 = n % 2
            P.tt("dve" if nn <= SUB_DVE_MAX else "pool", A[s2][:, 0:nn], Pb[pb][:, off:off + nn],
                 Pb[pb][:, off + 1:off + 1 + nn],
                 ALU.subtract, reads=([Pb_b[pb]] if ch["first"] else [Pb_b[pb], Pbh_b[pb]]),
                 writes=[A_b[s2]])

        def stage2(n):
            ch = chunks[n]
            nn = ch["n"]
            s2 = n % 2
            nt = nn // 128
            for t in range(nt):
                P.tr(ATp[:, t, :], A[s2][:, t * 128:(t + 1) * 128], ctx.identb[:], reads=[A_b[s2]],
                     writes=[ATp_b], sig=(t == nt - 1))
            P.copy("act", AT[s2][:, 0:nt, :], ATp[:, 0:nt, :], reads=[ATp_b], writes=[AT_b[s2]])

        def stage3(n):
            ch = chunks[n]
            hl, B, u0, nn, w, pr = ch["hl"], ch["B"], ch["u0"], ch["n"], ch["w"], ch["pr"]
            s2 = n % 2
            nt = nn // 128
            hs = slice(64 * hl, 64 * hl + 64)
            sl = ch["slot"]
            for t in range(nt):
                tg = u0 // 128 + t
                P.mm(oTp[hs, sl, :], vr[w][:, tg, hs], AT[s2][:, t, :],
                     ch["first"] and t == 0, ch["last"] and t == nt - 1,
                     reads=[v_b[w], AT_b[s2]], writes=[oTs_b[sl]], sig=(t == nt - 1))
            if ch["last"]:
                P.copy("act", oT[hs, pr, B * 128:(B + 1) * 128], oTp[hs, sl, :], reads=[oTs_b[sl]],
                       writes=[oT_b])

        if i == 0:
            bg = ctx.bg_items[:BG_ATTN_QUOTA]
            del ctx.bg_items[:BG_ATTN_QUOTA]
        else:
            bg = []
        bgcv = Converter(P, [_S(stg[0]), _S(stg[1])], [_S(sq[:].rearrange("p a b -> p (a b)"))])
        bgcv.b32 = xc_b
        bgcv.b16 = [sq_b]
        bgcv.engs = ("act",)

        for it in norm_items(0):
            it()
        for it in proj_items(0, 0):
            it()
        nxt_items = []
        nrm_items = []
        wo_q = []
        N = len(chunks)
        stage1(0)
        stage1s(0)
        stage1(1)
        stage1s(1)
        for n in range(N + 3):
            if n + 2 < N:
                stage1(n + 2)
            if 0 <= n - 3 < N:
                stage3(n - 3)
            if 0 <= n - 2 < N:
                stage2(n - 2)
            if n < N:
                stage1b(n)
            if n + 2 < N:
                stage1s(n + 2)
            if n < N:
                ch = chunks[n]
                s_, pr, kp = ch["s"], ch["pr"], ch["kp"]
                if kp == 4:
                    if pr + 1 < 8:
                        nxt_items = proj_items(pr + 1, (pr + 1) % 2)
                    elif s_ + 1 < NSEQ:
                        nxt_items = proj_items(0, 0)
                    if nxt_items:
                        nxt_items.pop(0)()
                if kp >= 4 and nxt_items:
                    nxt_items.pop(0)()
                if pr == 6 and kp == 20 and s_ + 1 < NSEQ:
                    nrm_items = norm_items(s_ + 1)
                if pr == 6 and kp >= 20 and kp % 4 == 0 and nrm_items:
                    nrm_items.pop(0)()
                if pr == 0 and kp == 2 and s_ > 0:
                    allw = wo_items(s_ - 1)
                    wo_q = allw[0:8] + allw[24:32] + allw[8:24]
                if pr == 0 and wo_q:
                    cnt = 8 if kp in (2, 3) else (2 if kp >= 4 else 0)
                    for _ in range(cnt):
                        if wo_q:
                            wo_q.pop(0)()
                if bg and pr < 6 and kp % 8 == 7:
                    bg.pop(0)(bgcv)
        assert not nxt_items and not nrm_items and not wo_q
        for it in wo_items(NSEQ - 1):
            it()
        while bg:
            bg.pop(0)(bgcv)
        P.run()


def phase_final(ctx):
    nc = ctx.nc
    with ExitStack() as st:
        sb = lambda n, s, d: st.enter_context(nc.sbuf_tensor(_uid(ctx, n), s, d))
        ps = lambda n, s, d: st.enter_context(nc.psum_tensor(_uid(ctx, n), s, d))
        P = Prog(ctx)
        xn = [sb(f"xn{k}", [128, 8, 512], F32) for k in range(3)]
        xn_b = [Buf() for _ in range(3)]
        sq = sb("sq", [128, 8, 512], BF16)
        sq_b = Buf()
        rstd = sb("rstd", [128, 512], F32)
        rstd_b = Buf()
        ostg = [sb(f"ostg{k}", [128, 1024], F32) for k in range(3)]
        ostg_b = [Buf() for _ in range(3)]
        pst = ps("pst", [128, 512], F32)
        pst_b = Buf()
        pt = [ps(f"pt{k}", [128, 1024], F32) for k in range(2)]
        pt_b = [Buf() for _ in range(2)]
        NT = ctx.TOK // 512
        oi = 0

        def load(n):
            b = n % 3
            P.dma("sp", xn[b][:], ctx.xT[:, :, n * 512:(n + 1) * 512], writes=[xn_b[b]])

        def front(n):
            b = n % 3
            emit_norm_stats(P, ctx, xn[b], xn_b[b], sq, sq_b, pst, pst_b, rstd, rstd_b)
            for k in range(8):
                P.stt("dve", xn[b][:, k, :], xn[b][:, k, :], ctx.gT[:, 64 + k:65 + k], rstd[:], ALU.mult, ALU.mult,
                      reads=[xn_b[b], rstd_b], writes=[xn_b[b]])

        load(0)
        if NT > 1:
            load(1)
        front(0)
        for n in range(NT):
            b = n % 3
            if n + 2 < NT:
                load(n + 2)
            if n + 1 < NT:
                front(n + 1)
            for jj in range(4):
                p = oi % 2
                s3 = oi % 3
                oi += 1
                for k in range(8):
                    P.tr(pt[p][:, k * 128:(k + 1) * 128], xn[b][:, k, jj * 128:(jj + 1) * 128], ctx.identf[:],
                         reads=[xn_b[b]], writes=[pt_b[p]], sig=(k == 7))
                P.copy("act" if oi % 2 == 0 else "dve", ostg[s3][:], pt[p][:], reads=[pt_b[p]],
                       writes=[ostg_b[s3]])
                t0 = n * 512 + jj * 128
                P.dma("sp", ctx.d_out[t0:t0 + 128, :], ostg[s3][:], reads=[ostg_b[s3]])
        P.run()


DBG_N = 0
SUB_DVE_MAX = 384
N_DUMMY = 0
BG_ATTN_QUOTA = 32
BLOCK_ORDER = [0, 15, 1, 14, 2, 13, 3, 12, 4, 11, 5, 10, 6, 9, 7, 8]


def build(tok=4 * SEQ, depth=DEPTH, debug_xT=False, debug_attn=False):
    nc = bass.Bass("TRN2", target_bir_lowering=False)
    ctx = Ctx()
    ctx.nc = nc
    ctx.TOK = tok
    ctx.depth = depth
    ctx.sb_layers = [i // 2 for i in range(depth) if i % 2 == 0]
    ctx.sgu_layers = [i // 2 for i in range(depth) if i % 2 == 1]
    ctx.mlp_layers = list(range(depth))
    ein = lambda n, s, d=F32: nc.dram_tensor(n, s, d, kind="ExternalInput").ap()
    ctx.d_x = ein("x", [tok, D])
    ctx.d_norm_mix = ein("norm_mix", [DEPTH, D])
    ctx.d_norm_mlp = ein("norm_mlp", [DEPTH, D])
    ctx.d_wqkv = ein("sb_wqkv", [2, D, 3 * D])
    ctx.d_wo = ein("sb_wo", [2, D, D])
    ctx.d_win = ein("sgu_win", [2, D, 2 * SGF])
    ctx.d_sgain = ein("sgu_gain", [2, SGF])
    ctx.d_sws = ein("sgu_ws", [2, 8, 128, 128])
    ctx.d_sbs = ein("sgu_bs", [2, 8, 128])
    ctx.d_wout = ein("sgu_wout", [2, SGF, D])
    ctx.d_w1 = ein("mlp_w1", [DEPTH, D, FF])
    ctx.d_w2 = ein("mlp_w2", [DEPTH, FF, D])
    ctx.d_final_norm = ein("final_norm", [D])
    ctx.d_identf = ein("c_identf", [128, 128])
    ctx.d_identb = ein("c_identb", [128, 128], BF16)
    ctx.d_onesb = ein("c_onesb", [128, 128], BF16)
    ctx.d_maskneg = ein("c_maskneg", [128, 128], BF16)
    ctx.d_tril = ein("c_tril", [128, 128])
    ctx.d_out = nc.dram_tensor("out", [tok, D], F32, kind="ExternalOutput").ap()
    ctx.xT = nc.dram_tensor("xT_scr", [128, 8, tok], F32, kind=("ExternalOutput" if debug_xT else "Internal")).ap()
    ctx.wqkvp = [nc.dram_tensor(f"wqkvp{j}", [128, 8, 3, 8, 128], BF16).ap() for j in range(2)]
    ctx.woutf = [nc.dram_tensor(f"woutf{j}", [128, 8, 16, 128], BF16).ap() for j in range(2)]
    ctx.w2f = [nc.dram_tensor(f"w2f{i}", [128, 8, 32, 128], BF16).ap() for i in range(DEPTH)]

    if debug_attn:
        eo = lambda n, s, d: nc.dram_tensor(n, s, d, kind="ExternalOutput").ap()
        ctx.dbg = dict(hrT=eo("d_hrT", [128, 8, 2048], BF16), qT=eo("d_qT", [128, 2048], BF16),
                       kT=eo("d_kT", [128, 2048], BF16), vr=eo("d_vr", [128, 16, 128], BF16),
                       G=eo("d_G", [128, 1024], F32), P=eo("d_P", [128, 2049], F32), A=eo("d_A", [128, 1024], BF16),
                       AT=eo("d_AT", [128, 8, 128], BF16), oT=eo("d_oT", [128, 8, 2048], BF16))
    with ExitStack() as gst:
        sbg = lambda n, s, d: gst.enter_context(nc.sbuf_tensor(n, s, d))
        ctx.identf = sbg("identf", [128, 128], F32)
        ctx.identb = sbg("identb", [128, 128], BF16)
        ctx.onesb = sbg("onesb", [128, 128], BF16)
        ctx.maskneg = sbg("maskneg", [128, 128], BF16)
        ctx.tril = sbg("tril", [128, 128], F32)
        ctx.zero1 = sbg("zero1", [128, 1], F32)
        ctx.epsc = sbg("epsc", [128, 1], F32)
        ctx.mhalf = sbg("mhalf", [128, 8], F32)
        ctx.gT = sbg("gT", [128, 72], F32)
        ctx.prog_sem = {e: Sem(gst.enter_context(nc.semaphore(f"prog_{e}"))) for e in ENGS}
        ctx.dma_sems = {
            "sp": [Sem(gst.enter_context(nc.semaphore(f"dsp{k}"))) for k in range(12)],
            "pool": [Sem(gst.enter_context(nc.semaphore(f"dpl{k}"))) for k in range(6)],
        }
        ctx.waited = {e: {} for e in ENGS}
        ctx.w1b = [nc.dram_tensor(f"w1b{i_}", [128, 8, 4096], BF16).ap() for i_ in range(DEPTH)]
        ctx.winb = [nc.dram_tensor(f"winb{j_}", [128, 8, 4096], BF16).ap() for j_ in range(2)]
        ident = lambda a: a

        def it_wqkv(j_):
            out = []
            for kc in range(8):
                def f(cv, kc=kc):
                    dst = [(ctx.wqkvp[j_][:, :, w_, kc, :],
                            (lambda a, w_=w_: a[:, w_ * 1024:(w_ + 1) * 1024].rearrange("p (q c) -> p q c", q=8)))
                           for w_ in range(3)]
                    cv.go(ctx.d_wqkv[j_, kc * 128:(kc + 1) * 128, :], [128, 3072],
                          gain=ctx.gT[:, 2 * j_ * 8 + kc:2 * j_ * 8 + kc + 1], dst_dram=dst)
                out.append(f)
            return out

        def it_fmaj(src_t, dst_t, nrow4):
            out = []
            for c4 in range(nrow4):
                def f(cv, c4=c4):
                    src = src_t[c4 * 512:(c4 + 1) * 512, :].rearrange("(c p) n -> p c n", p=128)
                    dst = [(dst_t[:, :, c4 * 4 + c_, :],
                            (lambda a, c_=c_: a[:, c_ * 1024:(c_ + 1) * 1024].rearrange("p (f n) -> p f n", f=8)))
                           for c_ in range(4)]
                    cv.go(src, [128, 4, 1024], dst_dram=dst)
                out.append(f)
            return out

        def it_rowmaj(src_t, dst_t, gcol0):
            out = []
            for kc in range(8):
                def f(cv, kc=kc):
                    cv.go(src_t[kc * 128:(kc + 1) * 128, :], [128, 4096],
                          gain=ctx.gT[:, gcol0 + kc:gcol0 + kc + 1], dst_dram=[(dst_t[:, kc, :], ident)])
                out.append(f)
            return out

        ctx.prep_items = it_wqkv(0)
        bg = []
        for i_ in range(depth):
            if i_ % 2 == 1:
                j_ = i_ // 2
                bg += it_rowmaj(ctx.d_win[j_], ctx.winb[j_], i_ * 8)
                bg += it_fmaj(ctx.d_wout[j_], ctx.woutf[j_], 4)
            elif i_ > 0:
                bg += it_wqkv(i_ // 2)
            bg += it_rowmaj(ctx.d_w1[i_], ctx.w1b[i_], 32 + i_ * 8)
            bg += it_fmaj(ctx.d_w2[i_], ctx.w2f[i_], 8)
        ctx.bg_items = bg
        phase_prep(ctx)
        for i in range(depth):
            if i % 2 == 0:
                phase_attn(ctx, i, i // 2)
            else:
                phase_sgu(ctx, i, i // 2)
            phase_mlp(ctx, i)
        phase_final(ctx)
    return nc


def host_consts():
    ii = np.arange(128)[:, None]
    rr = np.arange(128)[None, :]
    return {
        "c_identf": np.eye(128, dtype=np.float32),
        "c_identb": np.eye(128).astype(ml_dtypes.bfloat16),
        "c_onesb": np.ones((128, 128)).astype(ml_dtypes.bfloat16),
        "c_maskneg": np.where(rr <= ii, -30000.0, 0.0).astype(ml_dtypes.bfloat16),
        "c_tril": (rr <= ii).astype(np.float32),
    }


WEIGHT_KEYS = ("norm_mix", "norm_mlp", "sb_wqkv", "sb_wo", "sgu_win", "sgu_gain", "sgu_ws", "sgu_bs",
               "sgu_wout", "mlp_w1", "mlp_w2", "final_norm")


def kernel(**inputs):
    x = np.ascontiguousarray(np.asarray(inputs["x"], dtype=np.float32))
    B = x.shape[0]
    per = B // NCORES
    shared = {k: np.ascontiguousarray(np.asarray(inputs[k], dtype=np.float32)) for k in WEIGHT_KEYS}
    shared.update(host_consts())
    nc = build(tok=per * SEQ)
    in_maps = []
    for c in range(NCORES):
        m = dict(shared)
        m["x"] = x[c * per:(c + 1) * per].reshape(per * SEQ, D)
        in_maps.append(m)
    res = run_bass_kernel_spmd(nc, in_maps, core_ids=list(range(NCORES)))
    outs = [np.asarray(r["out"]).reshape(per, SEQ, D) for r in res.results]
    return np.concatenate(outs, axis=0).astype(np.float32)
```
